# Optimizing a Trainium2 kernel written in Bass

```python
import math
import jax, jax.numpy as jnp
from jax import lax
import numpy as np

D_MODEL = 1024
BATCH = 32
SEQ = 2048
DEPTH = 1

HEAD_DIM = 64
SB_HEADS = 8
MOBA_HEADS = 8
SB_WIDTH = SB_HEADS * HEAD_DIM
MOBA_WIDTH = MOBA_HEADS * HEAD_DIM
IN_WIDTH = 3 * SB_WIDTH + 3 * MOBA_WIDTH
SB_BLOCK = 128
MOBA_BLOCK = 256
MOBA_TOPK = 3
MOBA_Q_CHUNK = 16
NUM_BUCKETS = 32
MAX_EXACT = NUM_BUCKETS // 2
MAX_DISTANCE = 128
FFN_HIDDEN = int(math.ceil(8 * D_MODEL / 3 / 256) * 256)
PLE_DIM = 256
RMS_EPS = 1e-6
NEG = -1e30

kernel_name = "hybrid_stickbreak_moba_gated_block"


def rmsnorm(x, g):
    xf = x.astype(jnp.float32)
    y = xf * lax.rsqrt(jnp.mean(xf * xf, axis=-1, keepdims=True) + RMS_EPS)
    return (y * g.astype(jnp.float32)).astype(x.dtype)


def split_heads(t, n_heads):
    b, s, _ = t.shape
    return t.reshape(b, s, n_heads, HEAD_DIM).transpose(0, 2, 1, 3)


def merge_heads(t):
    b, h, s, d = t.shape
    return t.transpose(0, 2, 1, 3).reshape(b, s, h * d)


def t5_bucket(dist):
    n = jnp.maximum(dist, 0)
    nf = jnp.maximum(n, 1).astype(jnp.float32)
    large = MAX_EXACT + (jnp.log(nf / MAX_EXACT) / math.log(MAX_DISTANCE / MAX_EXACT)
                         * (NUM_BUCKETS - MAX_EXACT)).astype(jnp.int32)
    large = jnp.minimum(large, NUM_BUCKETS - 1)
    return jnp.where(n < MAX_EXACT, n, large)


def stick_breaking_attention(q, k, v):
    S = q.shape[2]
    scale = HEAD_DIM ** -0.5
    outs = []
    for i in range(S // SB_BLOCK):
        t0 = i * SB_BLOCK
        L = t0 + SB_BLOCK
        z = jnp.einsum('bhtd,bhsd->bhts', q[:, :, t0:L], k[:, :, :L]).astype(jnp.float32) * scale
        tpos = t0 + jnp.arange(SB_BLOCK)
        spos = jnp.arange(L)
        causal = spos[None, :] < tpos[:, None]
        log_fail = jnp.where(causal, jax.nn.log_sigmoid(-z), 0.0)
        suffix = lax.cumsum(log_fail, axis=3, reverse=True) - log_fail
        w = jnp.where(causal, jnp.exp(jax.nn.log_sigmoid(z) + suffix), 0.0)
        outs.append(jnp.einsum('bhts,bhsd->bhtd', w.astype(v.dtype), v[:, :, :L]))
    return jnp.concatenate(outs, axis=2)


_gather_blocks = jax.vmap(jax.vmap(lambda blk, idx: blk[idx]))


def moba_attention(q, k, v, rel_bias):
    B, H, S, dh = q.shape
    scale = HEAD_DIM ** -0.5
    nb = -(-S // MOBA_BLOCK)
    s_pad = nb * MOBA_BLOCK
    pad = ((0, 0), (0, 0), (0, s_pad - S), (0, 0))
    kb = jnp.pad(k, pad).reshape(B, H, nb, MOBA_BLOCK, dh)
    vb = jnp.pad(v, pad).reshape(B, H, nb, MOBA_BLOCK, dh)
    kbar = jnp.mean(kb.astype(jnp.float32), axis=3)
    gate = jnp.einsum('bhtd,bhnd->bhtn', q.astype(jnp.float32), kbar)
    cur_blk = jnp.arange(S) // MOBA_BLOCK
    past = jnp.arange(nb)[None, :] < cur_blk[:, None]
    gate = jnp.where(past, gate, -jnp.inf)
    k_sel = min(MOBA_TOPK, nb)
    _, sel = lax.top_k(gate, k_sel)
    sel_valid = sel < cur_blk[:, None]
    bias_t = rel_bias.T.astype(jnp.float32)
    h_idx = jnp.arange(H)[None, :, None, None, None]
    u = jnp.arange(MOBA_BLOCK)

    def chunk(ci):
        t0 = ci * MOBA_Q_CHUNK
        qc = lax.dynamic_slice_in_dim(q, t0, MOBA_Q_CHUNK, axis=2)
        selc = lax.dynamic_slice_in_dim(sel, t0, MOBA_Q_CHUNK, axis=2)
        validc = lax.dynamic_slice_in_dim(sel_valid, t0, MOBA_Q_CHUNK, axis=2)
        tpos = t0 + jnp.arange(MOBA_Q_CHUNK)
        c = t0 // MOBA_BLOCK
        own_k = lax.dynamic_index_in_dim(kb, c, axis=2, keepdims=False)
        own_v = lax.dynamic_index_in_dim(vb, c, axis=2, keepdims=False)
        spos_own = c * MOBA_BLOCK + u
        d_own = tpos[:, None] - spos_own[None, :]
        l_own = jnp.einsum('bhtd,bhsd->bhts', qc, own_k).astype(jnp.float32) * scale
        l_own = l_own + bias_t[:, t5_bucket(d_own)][None]
        l_own = jnp.where((d_own >= 0)[None, None], l_own, NEG)
        k_g = _gather_blocks(kb, selc)
        v_g = _gather_blocks(vb, selc)
        l_sel = jnp.einsum('bhtd,bhtjsd->bhtjs', qc, k_g).astype(jnp.float32) * scale
        d_sel = tpos[None, None, :, None, None] - (selc[..., None] * MOBA_BLOCK + u)
        l_sel = l_sel + bias_t[h_idx, t5_bucket(d_sel)]
        l_sel = jnp.where(validc[..., None], l_sel, NEG)
        kk = l_sel.shape[3]
        logits = jnp.concatenate([l_sel.reshape(B, H, MOBA_Q_CHUNK, kk * MOBA_BLOCK), l_own], axis=-1)
        probs = jax.nn.softmax(logits, axis=-1).astype(v.dtype)
        p_sel = probs[..., :kk * MOBA_BLOCK].reshape(B, H, MOBA_Q_CHUNK, kk, MOBA_BLOCK)
        p_own = probs[..., kk * MOBA_BLOCK:]
        return (jnp.einsum('bhtjs,bhtjsd->bhtd', p_sel, v_g)
                + jnp.einsum('bhts,bhsd->bhtd', p_own, own_v))

    outs = lax.map(chunk, jnp.arange(S // MOBA_Q_CHUNK))
    return outs.transpose(1, 2, 0, 3, 4).reshape(B, H, S, dh)


def setup_inputs(seed: int = 0) -> dict:
    key = jax.random.key(seed)
    ks = jax.random.split(key, 20)
    f32 = jnp.float32

    def w(k, shape, fan_in):
        return jax.random.normal(k, shape, f32) * (fan_in ** -0.5)

    def gain(k, shape):
        return 1.0 + 0.05 * jax.random.normal(k, shape, f32)

    return {
        "x": jax.random.normal(ks[0], (BATCH, SEQ, D_MODEL), f32),
        "p": jax.random.normal(ks[1], (DEPTH, BATCH, SEQ, PLE_DIM), f32),
        "ln_mix_g": gain(ks[2], (DEPTH, D_MODEL)),
        "w_in": w(ks[3], (DEPTH, D_MODEL, IN_WIDTH), D_MODEL),
        "w_gate": w(ks[4], (DEPTH, D_MODEL, 2 * D_MODEL), D_MODEL),
        "b_gate": 0.02 * jax.random.normal(ks[5], (DEPTH, 2 * D_MODEL), f32),
        "w_branch_sb": w(ks[6], (DEPTH, SB_WIDTH, D_MODEL), SB_WIDTH),
        "w_branch_moba": w(ks[7], (DEPTH, MOBA_WIDTH, D_MODEL), MOBA_WIDTH),
        "w_out": w(ks[8], (DEPTH, D_MODEL, D_MODEL), D_MODEL),
        "rel_bias": 0.5 * jax.random.normal(ks[9], (NUM_BUCKETS, MOBA_HEADS), f32),
        "ln_ffn_g": gain(ks[10], (DEPTH, D_MODEL)),
        "w_ffn_gate": w(ks[11], (DEPTH, D_MODEL, FFN_HIDDEN), D_MODEL),
        "w_ffn_up": w(ks[12], (DEPTH, D_MODEL, FFN_HIDDEN), D_MODEL),
        "w_ffn_down": w(ks[13], (DEPTH, FFN_HIDDEN, D_MODEL), FFN_HIDDEN),
        "ln_ple_g": gain(ks[14], (DEPTH, D_MODEL)),
        "w_ple_gate": w(ks[15], (DEPTH, D_MODEL, D_MODEL), D_MODEL),
        "w_ple_proj": w(ks[16], (DEPTH, PLE_DIM, D_MODEL), PLE_DIM),
        "final_g": gain(ks[17], (D_MODEL,)),
    }


def reference(x, p, ln_mix_g, w_in, w_gate, b_gate, w_branch_sb, w_branch_moba, w_out,
              rel_bias, ln_ffn_g, w_ffn_gate, w_ffn_up, w_ffn_down, ln_ple_g, w_ple_gate,
              w_ple_proj, final_g):
    for i in range(DEPTH):
        h = rmsnorm(x, ln_mix_g[i])
        proj = h @ w_in[i]
        q_sb, k_sb, v_sb, q_mb, k_mb, v_mb = jnp.split(
            proj, np.cumsum([SB_WIDTH] * 3 + [MOBA_WIDTH] * 2).tolist(), axis=-1)
        o_sb = stick_breaking_attention(split_heads(q_sb, SB_HEADS), split_heads(k_sb, SB_HEADS),
                                        split_heads(v_sb, SB_HEADS))
        o_mb = moba_attention(split_heads(q_mb, MOBA_HEADS), split_heads(k_mb, MOBA_HEADS),
                              split_heads(v_mb, MOBA_HEADS), rel_bias)
        y_sb = merge_heads(o_sb) @ w_branch_sb[i]
        y_mb = merge_heads(o_mb) @ w_branch_moba[i]
        g_sb, g_mb = jnp.split(jax.nn.sigmoid(h @ w_gate[i] + b_gate[i]), 2, axis=-1)
        x = x + (g_sb * y_sb + g_mb * y_mb) @ w_out[i]
        h2 = rmsnorm(x, ln_ffn_g[i])
        x = x + (jax.nn.silu(h2 @ w_ffn_gate[i]) * (h2 @ w_ffn_up[i])) @ w_ffn_down[i]
        g_ple = jax.nn.sigmoid(rmsnorm(x, ln_ple_g[i]) @ w_ple_gate[i])
        x = x + g_ple * (p[i] @ w_ple_proj[i])
    return rmsnorm(x, final_g)
```

```python
import math
from contextlib import ExitStack

import numpy as np
import concourse.bass as bass
import concourse.mybir as mybir
from concourse.bass_utils import run_bass_kernel_spmd

F32 = mybir.dt.float32
BF16 = mybir.dt.bfloat16
AF = mybir.ActivationFunctionType
ALU = mybir.AluOpType
AX = mybir.AxisListType

D = 1024
S = 2048
BATCH = 32
NCORES = 8
HD = 64
NH = 8
FF = 2816
NFC = FF // 128
PLE = 256
NEG = -30000.0
EPS = 1e-6
LV = 768
TBW = 640
NSLOT = 7
NDUM_SB = 0
NDUM_MB = 0


class Buf:
    __slots__ = ("name", "w", "r")

    def __init__(self, name):
        self.name = name
        self.w = None
        self.r = {}


class DSem:
    def __init__(self, handle, key):
        self.h = handle
        self.key = key
        self.count = 0


class Eng:
    def __init__(self, name, handle, sem):
        self.name = name
        self.h = handle
        self.sem = sem
        self.key = "E_" + name
        self.count = 0
        self.known = {}


class Tracker:
    def __init__(self, nc, es):
        self.nc = nc
        self.es = es
        self.E = {}
        for name, h in (("pe", nc.tensor), ("act", nc.scalar), ("dve", nc.vector),
                        ("pool", nc.gpsimd), ("sp", nc.sync)):
            self.E[name] = Eng(name, h, es.enter_context(nc.semaphore("sem_" + name)))
        self.dsems = []
        self.bufs = []

    def buf(self, name):
        b = Buf(name)
        self.bufs.append(b)
        return b

    def bufs_n(self, name, n):
        return [self.buf("%s%d" % (name, i)) for i in range(n)]

    def dsem(self, name):
        d = DSem(self.es.enter_context(self.nc.semaphore("d_" + name)), "D_" + name)
        self.dsems.append(d)
        return d

    def _waits(self, E, reads, writes):
        need = {}

        def req(ev, same_ok):
            if ev is None:
                return
            k, sem, val = ev
            if k == E.key and E.name == "pe":
                return
            if need.get(k, (None, 0))[1] < val:
                need[k] = (sem, val)

        for b in reads:
            req(b.w, False)
        for b in writes:
            req(b.w, True)
            for ev in b.r.values():
                req(ev, True)
        for k, (sem, val) in need.items():
            if E.known.get(k, 0) >= val:
                continue
            E.h.wait_ge(sem, val)
            E.known[k] = val

    def _record(self, ev, reads, writes):
        k = ev[0]
        for b in reads:
            old = b.r.get(k)
            if old is None or old[2] < ev[2]:
                b.r[k] = ev
        for b in writes:
            b.w = ev
            b.r = {}

    def op(self, eng, fn, reads=(), writes=(), inc=True):
        E = self.E[eng]
        self._waits(E, reads, writes)
        ins = fn()
        if inc:
            E.count += 1
            ins.then_inc(E.sem, 1)
            ev = (E.key, E.sem, E.count)
        else:
            ev = (E.key, E.sem, E.count + 1)
        self._record(ev, reads, writes)
        return ins

    def dma(self, q, out, in_, sem, reads=(), writes=(), **kw):
        E = self.E[q]
        self._waits(E, reads, writes)
        ins = E.h.dma_start(out=out, in_=in_, **kw)
        sem.count += 16
        ins.then_inc(sem.h, 16)
        ev = (sem.key, sem.h, sem.count)
        self._record(ev, reads, writes)
        return ins

    def barrier(self, skip=(), keep=()):
        evs = [(e.key, e.sem, e.count) for e in self.E.values() if e.count > 0]
        evs += [(d.key, d.h, d.count) for d in self.dsems if d.count > 0]
        for E in self.E.values():
            if E.name in skip:
                continue
            for k, sem, val in evs:
                if k == E.key:
                    continue
                if E.known.get(k, 0) >= val:
                    continue
                E.h.wait_ge(sem, val)
                E.known[k] = val
        keep_ids = set(id(b) for b in keep)
        for b in self.bufs:
            if id(b) in keep_ids:
                continue
            b.w = None
            b.r = {}


def _bucket_table():
    d = np.arange(LV) - 127
    n = np.maximum(d, 0)
    nf = np.maximum(n, 1).astype(np.float32)
    large = 16 + (np.log(nf / np.float32(16)).astype(np.float32) / np.float32(math.log(8.0))
                  * np.float32(16)).astype(np.int32)
    large = np.minimum(large, 31)
    bk = np.where(n < 16, n, large)
    oh = np.zeros((33, LV), np.float32)
    for i in range(LV):
        if d[i] < 0:
            oh[32, i] = 1.0
        else:
            oh[bk[i], i] = 1.0
    return oh


def _consts():
    p = np.arange(128)[:, None]
    j = np.arange(128)[None, :]
    c = {}
    c["c_ident"] = np.eye(128, dtype=np.float32)
    c["c_negui"] = np.where(p >= j, -1.0, 0.0).astype(np.float32)
    c["c_negones"] = -np.ones((128, 128), np.float32)
    c["c_ones"] = np.ones((128, 128), np.float32)
    c["c_mtri"] = np.where(j <= p, NEG, 0.0).astype(np.float32)
    c["c_oh"] = _bucket_table()
    kind = np.zeros((8, S), np.float32)
    for n in range(8):
        kind[n, n * 256:(n + 1) * 256] = 1.0
    c["c_kind"] = kind
    return c


def build(nseq, dbg=None):
    dbg = dbg or set()
    nc = bass.Bass("TRN2", target_bir_lowering=False)

    def din(name, shape):
        return nc.dram_tensor(name, list(shape), F32, kind="ExternalInput")

    x_d = din("x", [nseq, S, D])
    p_d = din("p", [nseq, S, PLE])
    ln_mix_d = din("ln_mix_g", [D])
    w_in_d = din("w_in", [D, 3072])
    w_gate_d = din("w_gate", [D, 2048])
    b_gate_d = din("b_gate", [2048])
    w_bsb_d = din("w_branch_sb", [512, D])
    w_bmb_d = din("w_branch_moba", [512, D])
    w_out_d = din("w_out", [D, D])
    relb_d = din("rel_bias", [32, 8])
    ln_ffn_d = din("ln_ffn_g", [D])
    w_fg_d = din("w_ffn_gate", [D, FF])
    w_fu_d = din("w_ffn_up", [D, FF])
    w_fd_d = din("w_ffn_down", [FF, D])
    ln_ple_d = din("ln_ple_g", [D])
    w_pg_d = din("w_ple_gate", [D, D])
    w_pp_d = din("w_ple_proj", [PLE, D])
    fin_d = din("final_g", [D])
    c_ident_d = din("c_ident", [128, 128])
    c_negui_d = din("c_negui", [128, 128])
    c_negones_d = din("c_negones", [128, 128])
    c_ones_d = din("c_ones", [128, 128])
    c_mtri_d = din("c_mtri", [128, 128])
    c_oh_d = din("c_oh", [33, LV])
    c_kind_d = din("c_kind", [8, S])
    y_d = nc.dram_tensor("y", [nseq, S, D], F32, kind="ExternalOutput")
    vd_d = nc.dram_tensor("vd_scr", [8 * LV], F32, kind="Internal")
    fd_d = nc.dram_tensor("fd_scr", [8 * 128 * LV], F32, kind="Internal")
    tbd_d = nc.dram_tensor("tbd_scr", [128, 8 * TBW], BF16, kind="Internal")
    kindd_d = nc.dram_tensor("kindd_scr", [8, S], BF16, kind="Internal")
    dbg_t = {}

    with ExitStack() as es:
        T = Tracker(nc, es)
        sb = lambda name, shape, dt: es.enter_context(nc.sbuf_tensor(name, list(shape), dt))

        ident = sb("ident", [128, 128], BF16)
        negui = sb("negui", [128, 128], BF16)
        negones = sb("negones", [128, 128], BF16)
        ones_b = sb("ones_b", [128, 128], BF16)
        mtri = sb("mtri", [128, 128], BF16)
        gmixT = sb("gmixT", [128, 8], F32)
        gffnT = sb("gffnT", [128, 8], F32)
        gpleT = sb("gpleT", [128, 8], F32)
        bgT = sb("bgT", [128, 16], F32)
        b31 = sb("b31", [128, 8], F32)
        maskpad = sb("maskpad", [128, 3, 4, 72], BF16)
        hT = sb("hT", [128, 8, S], BF16)
        oT_sb = sb("oT_sb", [128, 4, S], BF16)
        oT_mb = sb("oT_mb", [128, 4, S], BF16)
        wp = sb("wp", [128, NSLOT, 4096], BF16)
        ps = [es.enter_context(nc.psum_tensor("ps%d" % i, [128, 512], F32)) for i in range(8)]

        ps_b = T.bufs_n("ps", 8)
        wp_b = T.bufs_n("wp", NSLOT)
        wp_s = [T.dsem("wp%d" % i) for i in range(NSLOT)]
        hT_b = T.bufs_n("hT", 16)
        oTsb_b = [T.bufs_n("oTsb%d_" % j, 4) for j in range(4)]
        oTmb_b = [T.bufs_n("oTmb%d_" % j, 4) for j in range(4)]
        cst_b = T.buf("consts")
        cst_s = T.dsem("consts")
        cst2_s = T.dsem("consts2")
        out_s = [T.dsem("out%d" % i) for i in range(4)]
        xs_s = [T.dsem("xs%d" % i) for i in range(4)]
        pt_s = [T.dsem("pt%d" % i) for i in range(4)]
        misc_s = T.dsem("misc")
        tb_s = T.dsem("tbload")
        kind_s = T.dsem("kind")
        gfin_s = T.dsem("gfin")
        dbg_s = T.dsem("dbg")

        state = {"bank": 0, "wslot": 0}

        def bank():
            i = state["bank"]
            state["bank"] = (i + 1) % 8
            return i

        rolec = {}

        def rbank(role, banks):
            i = rolec.get(role, 0)
            rolec[role] = i + 1
            return banks[i % len(banks)]

        def dump(name, ap, bufs, shape, dt=F32):
            if name not in dbg:
                return
            t = nc.dram_tensor("dbg_" + name, list(shape), dt, kind="ExternalOutput")
            dbg_t[name] = t
            T.dma("sp", t.ap(), ap, dbg_s, reads=bufs)

        wcache = {}
        wkeep = list(wp_b)

        def wload(wd, ncols_total, row_chunk0, nk, col0, ncols):
            s = state["wslot"]
            state["wslot"] = (s + 1) % NSLOT
            dst = wp[:, s, 0:nk * ncols].rearrange("p (k n) -> p k n", k=nk)
            key = (wd.name, row_chunk0, nk, col0, ncols)
            if key not in wcache:
                src = bass.AP(wd, row_chunk0 * 128 * ncols_total + col0,
                              [[ncols_total, 128], [128 * ncols_total, nk], [1, ncols]])
                T.dma("pool", dst, src, wp_s[s], writes=[wp_b[s]])
                n = len(wcache)
                scr = nc.dram_tensor("wscr%d" % n, [128, nk * ncols], BF16, kind="Internal")
                sb_ = T.buf("wscr%d" % n)
                wkeep.append(sb_)
                ss_ = T.dsem("wscr%d" % n)
                T.dma("sp", scr.ap(), wp[:, s, 0:nk * ncols], ss_, reads=[wp_b[s]], writes=[sb_])
                wcache[key] = (scr, sb_)
            else:
                scr, sb_ = wcache[key]
                T.dma("pool", wp[:, s, 0:nk * ncols], scr.ap(), wp_s[s], reads=[sb_], writes=[wp_b[s]])
            return dst, wp_b[s]

        def cload(dst, src_d):
            T.dma("pool", dst, src_d, cst_s, writes=[cst_b])

        cload(ident[:], c_ident_d.ap())
        cload(negui[:], c_negui_d.ap())
        cload(negones[:], c_negones_d.ap())
        cload(ones_b[:], c_ones_d.ap())
        cload(mtri[:], c_mtri_d.ap())
        for gt, gd in ((gmixT, ln_mix_d), (gffnT, ln_ffn_d), (gpleT, ln_ple_d)):
            T.dma("sp", gt[:], bass.AP(gd, 0, [[1, 128], [128, 8]]), cst2_s, writes=[cst_b],
                  allow_slow_non_contiguous=True)
        T.dma("sp", bgT[:], bass.AP(b_gate_d, 0, [[1, 128], [128, 16]]), cst2_s, writes=[cst_b],
              allow_slow_non_contiguous=True)
        T.dma("sp", b31[:], bass.AP(relb_d, 31 * 8, [[0, 128], [1, 8]]), cst2_s, writes=[cst_b])
        T.op("dve", lambda: nc.vector.memset(maskpad[:], 0.0), writes=[cst_b])
        with ExitStack() as es0:
            rbx = es0.enter_context(nc.sbuf_tensor("rbx", [33, 8], F32))
            ohs = es0.enter_context(nc.sbuf_tensor("ohs", [33, LV], F32))
            vecsb = es0.enter_context(nc.sbuf_tensor("vecsb", [8, LV], F32))
            TB = es0.enter_context(nc.sbuf_tensor("TBtmp", [128, 8, TBW], BF16))
            kindt = es0.enter_context(nc.sbuf_tensor("kindtmp", [8, S], BF16))
            T.dma("pool", kindt[:], c_kind_d.ap(), cst_s, writes=[cst_b])
            T.op("dve", lambda: nc.vector.memset(rbx[32:33, :], NEG), writes=[cst_b])
            T.dma("sp", rbx[0:32, :], relb_d.ap(), cst2_s, writes=[cst_b])
            T.dma("sp", ohs[:], c_oh_d.ap(), cst2_s, writes=[cst_b])
            T.barrier()
            T.op("pe", lambda: nc.tensor.matmul(ps[0][0:8, 0:512], lhsT=rbx[0:33, :], rhs=ohs[0:33, 0:512],
                                                start=True, stop=True), writes=[ps_b[0]])
            T.op("pe", lambda: nc.tensor.matmul(ps[1][0:8, 0:LV - 512], lhsT=rbx[0:33, :], rhs=ohs[0:33, 512:LV],
                                                start=True, stop=True), writes=[ps_b[1]])
            T.op("act", lambda: nc.scalar.copy(out=vecsb[:, 0:512], in_=ps[0][0:8, 0:512]),
                 reads=[ps_b[0]], writes=[cst_b])
            T.op("act", lambda: nc.scalar.copy(out=vecsb[:, 512:LV], in_=ps[1][0:8, 0:LV - 512]),
                 reads=[ps_b[1]], writes=[cst_b])
            T.barrier()
            T.dma("sp", bass.AP(vd_d, 0, [[LV, 8], [1, LV]]), vecsb[:], cst2_s, reads=[cst_b])
            T.barrier()
            T.dma("sp", bass.AP(fd_d, 0, [[128 * LV, 8], [LV, 128], [1, LV]]),
                  bass.AP(vd_d, 0, [[LV, 8], [0, 128], [1, LV]]), cst2_s)
            T.barrier()
            for h in range(8):
                T.dma("pool", TB[:, h, :], bass.AP(fd_d, h * 128 * LV + 127, [[LV - 1, 128], [1, TBW]]),
                      cst_s, writes=[cst_b])
            T.barrier()
            T.dma("sp", tbd_d.ap(), TB[:].rearrange("p h j -> p (h j)"), cst2_s, reads=[cst_b])
            T.dma("sp", kindd_d.ap(), kindt[:], cst2_s, reads=[cst_b])
            T.barrier()

        def evac_copy(which, out, in_, reads, writes, scale=None):
            if which == "act":
                if scale is None:
                    T.op("act", lambda: nc.scalar.copy(out=out, in_=in_), reads=reads, writes=writes)
                else:
                    T.op("act", lambda: nc.scalar.activation(out=out, in_=in_, func=AF.Copy, scale=scale),
                         reads=reads, writes=writes)
            else:
                if scale is None:
                    T.op("dve", lambda: nc.vector.tensor_copy(out=out, in_=in_), reads=reads, writes=writes)
                else:
                    T.op("dve", lambda: nc.vector.tensor_scalar(out=out, in0=in_, scalar1=scale, scalar2=None,
                                                                op0=ALU.mult), reads=reads, writes=writes)

        for sq in range(nseq):
            def norm_rstd(xt, xt_b, ss, rstd, st_b, junk, junk_b):
                T.op("act", lambda: nc.scalar.activation(out=junk, in_=xt, func=AF.Square, accum_out=ss),
                     reads=[xt_b], writes=[junk_b, st_b])
                T.op("dve", lambda: nc.vector.tensor_scalar(out=rstd, in0=ss, scalar1=1.0 / D, scalar2=EPS,
                                                            op0=ALU.mult, op1=ALU.add),
                     reads=[st_b], writes=[st_b])
                T.op("act", lambda: nc.scalar.activation(out=rstd, in_=rstd, func=AF.Sqrt),
                     reads=[st_b], writes=[st_b])
                T.op("dve", lambda: nc.vector.reciprocal(out=rstd, in_=rstd),
                     reads=[st_b], writes=[st_b])

            def norm_chain(xt, xt_b, ss, rstd, st_b, junk, junk_b, xn, xn_b):
                norm_rstd(xt, xt_b, ss, rstd, st_b, junk, junk_b)
                T.op("dve", lambda: nc.vector.tensor_scalar(out=xn, in0=xt, scalar1=rstd, scalar2=None,
                                                            op0=ALU.mult),
                     reads=[xt_b, st_b], writes=[xn_b])

            def norm_tr(xn, xn_b, gT, dst, dst_b):
                bi = bank()
                pst = ps[bi][:].bitcast(BF16)
                for c in range(8):
                    T.op("pe", lambda c=c: nc.tensor.transpose(out=pst[:, c * 128:(c + 1) * 128],
                                                               in_=xn[:, c * 128:(c + 1) * 128],
                                                               identity=ident[:]),
                         reads=[xn_b], writes=[ps_b[bi]], inc=(c == 7))
                T.op("dve", lambda: nc.vector.tensor_tensor(
                    out=dst, in0=pst[:, 0:1024].rearrange("p (c t) -> p c t", c=8),
                    in1=gT[:, :].unsqueeze(2).broadcast_to([128, 8, 128]), op=ALU.mult),
                     reads=[ps_b[bi]], writes=[dst_b])

            def norm_transpose(xt, xt_b, ss, rstd, st_b, junk, junk_b, xn, xn_b, gT, dst, dst_b):
                norm_chain(xt, xt_b, ss, rstd, st_b, junk, junk_b, xn, xn_b)
                norm_tr(xn, xn_b, gT, dst, dst_b)

            def proj_fm(wblk, wb, col0, M, dst_fn, scale, k=8, groups=range(4)):
                for g in groups:
                    bi = bank()
                    for c in range(k):
                        T.op("pe", lambda c=c: nc.tensor.matmul(ps[bi][0:M, :], lhsT=wblk[:, c, col0:col0 + M],
                                                                rhs=hT[:, c, g * 512:(g + 1) * 512],
                                                                start=(c == 0), stop=(c == k - 1)),
                             reads=[wb] + hT_b[4 * g:4 * g + 4], writes=[ps_b[bi]], inc=(c == k - 1))
                    out, ob = dst_fn(g)
                    evac_copy("act" if g % 2 == 0 else "dve", out, ps[bi][0:M, :], [ps_b[bi]], [ob], scale)

            def proj_tm(wblk, wb, dst, dst_b, aug=False, tiles=range(16)):
                for i in tiles:
                    bi = bank()
                    for c in range(8):
                        T.op("pe", lambda c=c: nc.tensor.matmul(ps[bi][:, :], lhsT=hT[:, c, i * 128:(i + 1) * 128],
                                                                rhs=wblk[:, c, 0:512],
                                                                start=(c == 0), stop=(c == 7)),
                             reads=[wb, hT_b[i]], writes=[ps_b[bi]], inc=(c == 7))
                    if aug:
                        psv = ps[bi][:, :].rearrange("p (j e d) -> p j e d", j=4, e=2)
                        evac_copy("act", dst[:, i, :, 0, 0:64], psv[:, :, 0, :], [ps_b[bi]], [dst_b[i]])
                        evac_copy("dve", dst[:, i, :, 1, 64:128], psv[:, :, 1, :], [ps_b[bi]], [dst_b[i]])
                    else:
                        evac_copy("act" if i % 2 == 0 else "dve", dst[:, i, :], ps[bi][:, :], [ps_b[bi]], [dst_b[i]])

            with ExitStack() as esC:
                sa = lambda name, shape, dt: esC.enter_context(nc.sbuf_tensor("%s_%d" % (name, sq), list(shape), dt))
                qT = sa("qTz", [128, 8, S], BF16)
                kT = sa("kT", [128, 4, S], BF16)
                vS = sa("vS", [128, 16, 512], BF16)
                qT_b = [T.bufs_n("qT%d_" % j, 4) for j in range(4)]
                kT_b = [T.bufs_n("kT%d_" % j, 4) for j in range(4)]
                vS_b = T.bufs_n("vS", 16)

                with ExitStack() as esA:
                    sa2 = lambda name, shape, dt: esA.enter_context(
                        nc.sbuf_tensor("%s_%d" % (name, sq), list(shape), dt))
                    xs = sa2("xsA", [128, 2, D], F32)
                    xn = sa2("xnA", [128, 3, D], BF16)
                    stt = sa2("sttA", [128, 8], F32)
                    xs_b = T.bufs_n("xsA", 2)
                    xn_b = T.bufs_n("xnA", 3)
                    st_b = T.bufs_n("stA", 4)
                    T.op("dve", lambda: nc.vector.memset(qT[64:128, 0:8:2, :], 0.0), writes=sum(qT_b, []))
                    T.op("dve", lambda: nc.vector.memset(qT[0:64, 1:8:2, :], 0.0), writes=sum(qT_b, []))
                    wq, wqb = wload(w_in_d, 3072, 0, 8, 0, 512)
                    wk, wkb = wload(w_in_d, 3072, 0, 8, 512, 512)
                    wv, wvb = wload(w_in_d, 3072, 0, 8, 1024, 512)

                    def projA(g):
                        for j in range(4):
                            bi = bank()
                            for c in range(8):
                                T.op("pe", lambda c=c: nc.tensor.matmul(ps[bi][:, :], lhsT=wq[:, c, j * 128:(j + 1) * 128],
                                                                        rhs=hT[:, c, g * 512:(g + 1) * 512],
                                                                        start=(c == 0), stop=(c == 7)),
                                     reads=[wqb] + hT_b[4 * g:4 * g + 4], writes=[ps_b[bi]], inc=(c == 7))
                            evac_copy("act", qT[0:64, 2 * j, g * 512:(g + 1) * 512], ps[bi][0:64, :],
                                      [ps_b[bi]], [qT_b[j][g]], 0.125)
                            evac_copy("dve", qT[64:128, 2 * j + 1, g * 512:(g + 1) * 512], ps[bi][64:128, :],
                                      [ps_b[bi]], [qT_b[j][g]], 0.125)
                        for j in range(4):
                            proj_fm(wk, wkb, j * 128, 128,
                                    lambda g_, j=j: (kT[:, j, g_ * 512:(g_ + 1) * 512], kT_b[j][g_]), None,
                                    groups=[g])
                        proj_tm(wv, wvb, vS, vS_b, tiles=range(4 * g, 4 * g + 4))

                    def trA(i):
                        norm_tr(xn[:, i % 3, :], xn_b[i % 3], gmixT, hT[:, :, i * 128:(i + 1) * 128], hT_b[i])
                        if i % 4 == 3:
                            projA(i // 4)

                    pend = None
                    for i in range(16):
                        sl = i % 2
                        T.dma("sp", xs[:, sl, :], x_d.ap()[sq, i * 128:(i + 1) * 128, :], xs_s[sl],
                              writes=[xs_b[sl]])
                        norm_chain(xs[:, sl, :], xs_b[sl], stt[:, i % 4:i % 4 + 1], stt[:, 4 + i % 4:5 + i % 4],
                                   st_b[i % 4], xn[:, i % 3, :], xn_b[i % 3], xn[:, i % 3, :], xn_b[i % 3])
                        if pend is not None:
                            trA(pend)
                        pend = i
                    trA(pend)
                    T.barrier(skip=("pool",), keep=wkeep)
                if sq == 0:
                    dump("hT", hT[:], hT_b, [128, 8, S], BF16)
                    dump("kT", kT[:], sum(kT_b, []), [128, 4, S], BF16)
                    dump("vS", vS[:], vS_b, [128, 16, 512], BF16)

                lfn = sa("lfn", [128, 4, 512], BF16)
                ww = sa("ww", [128, 3, 512], BF16)
                car = sa("car", [128, 2, 512], F32)
                lfn_b = T.bufs_n("lfn", 4)
                ww_b = T.bufs_n("ww", 3)
                car_b = T.bufs_n("car", 2)

                items = []
                for h in range(NH):
                    for g in range(4):
                        kbs = list(range(4 * g + 3, -1, -1))
                        bo = rbank("sbo", [0, 1])
                        cs = rbank("sbcar", [0, 1])
                        for idx, kb in enumerate(kbs):
                            r = kb - 4 * g
                            items.append(dict(h=h, g=g, idx=idx, kb=kb, n=len(kbs), bo=bo, cs=cs, diag=(r >= 0),
                                              c0=(128 * r if r > 0 else 0), k=len(items)))

                def sbA(it):
                    h, g, kb, c0, k = it["h"], it["g"], it["kb"], it["c0"], it["k"]
                    j, po, t0 = h // 2, (h % 2) * 64, g * 512
                    bz = 2 + k % 2
                    kap = kT[:, j, kb * 128:(kb + 1) * 128]
                    qap = qT[:, h, t0 + c0:t0 + 512]
                    rd = [kT_b[j][kb // 4], qT_b[j][g]]
                    T.op("pe", lambda: nc.tensor.matmul(ps[bz][:, c0:512], lhsT=kap, rhs=qap,
                                                        start=True, stop=not it["diag"]),
                         reads=rd, writes=[ps_b[bz]], inc=not it["diag"])
                    if it["diag"]:
                        T.op("pe", lambda: nc.tensor.matmul(ps[bz][:, c0:c0 + 128], lhsT=ident[:],
                                                            rhs=mtri[:], start=False, stop=True),
                             writes=[ps_b[bz]])

                def sbB(it):
                    c0, k = it["c0"], it["k"]
                    bz, se, sl = 2 + k % 2, k % 2, k % 4
                    T.op("act", lambda: nc.scalar.activation(out=ps[bz][:, c0:512], in_=ps[bz][:, c0:512],
                                                             func=AF.Exp),
                         reads=[ps_b[bz]], writes=[ps_b[bz]])
                    T.op("act", lambda: nc.scalar.activation(out=lfn[:, sl, c0:512], in_=ps[bz][:, c0:512],
                                                             func=AF.Ln, bias=1.0, scale=1.0),
                         reads=[ps_b[bz]], writes=[lfn_b[sl]])

                def sbC(it):
                    h, g, kb, c0, k = it["h"], it["g"], it["kb"], it["c0"], it["k"]
                    j, po, t0 = h // 2, (h % 2) * 64, g * 512
                    bt, bc, sl = 4 + k % 2, 6, k % 4
                    kap = kT[:, j, kb * 128:(kb + 1) * 128]
                    qap = qT[:, h, t0 + c0:t0 + 512]
                    rd = [kT_b[j][kb // 4], qT_b[j][g]]
                    T.op("pe", lambda: nc.tensor.matmul(ps[bt][:, c0:512], lhsT=kap, rhs=qap,
                                                        start=True, stop=False),
                         reads=rd, writes=[ps_b[bt]], inc=False)
                    if it["diag"]:
                        T.op("pe", lambda: nc.tensor.matmul(ps[bt][:, c0:c0 + 128], lhsT=ident[:],
                                                            rhs=mtri[:], start=False, stop=False),
                             writes=[ps_b[bt]], inc=False)
                    T.op("pe", lambda: nc.tensor.matmul(ps[bt][:, c0:512], lhsT=negui[:],
                                                        rhs=lfn[:, sl, c0:512], start=False, stop=True),
                         reads=[lfn_b[sl]], writes=[ps_b[bt]])
                    if it["idx"] < it["n"] - 1:
                        T.op("pe", lambda: nc.tensor.matmul(ps[bc][:, c0:512], lhsT=negones[:],
                                                            rhs=lfn[:, sl, c0:512], start=True, stop=True),
                             reads=[lfn_b[sl]], writes=[ps_b[bc]])
                    for _ in range(NDUM_SB):
                        T.op("pe", lambda: nc.tensor.matmul(ps[7][:, :], lhsT=ident[:], rhs=hT[:, 0, 0:512],
                                                            start=True, stop=True),
                             writes=[ps_b[7]], inc=False)

                def sbD(it):
                    c0, k, cs, idx = it["c0"], it["k"], it["cs"], it["idx"]
                    bt, bc, st = 4 + k % 2, 6, k % 3
                    if idx > 0:
                        T.op("dve", lambda: nc.vector.tensor_tensor(out=ps[bt][:, c0:512], in0=ps[bt][:, c0:512],
                                                                    in1=car[:, cs, c0:512], op=ALU.add),
                             reads=[ps_b[bt], car_b[cs]], writes=[ps_b[bt]])
                    if idx < it["n"] - 1:
                        if idx == 0:
                            if c0 > 0:
                                T.op("dve", lambda: nc.vector.memset(car[:, cs, 0:c0], 0.0), writes=[car_b[cs]])
                            T.op("dve", lambda: nc.vector.tensor_copy(out=car[:, cs, c0:512], in_=ps[bc][:, c0:512]),
                                 reads=[ps_b[bc]], writes=[car_b[cs]])
                        else:
                            T.op("dve", lambda: nc.vector.tensor_tensor(out=car[:, cs, c0:512],
                                                                        in0=ps[bc][:, c0:512],
                                                                        in1=car[:, cs, c0:512], op=ALU.add),
                                 reads=[ps_b[bc], car_b[cs]], writes=[car_b[cs]])

                def sbE(it):
                    c0, k, idx = it["c0"], it["k"], it["idx"]
                    bt, st, sw = 4 + k % 2, k % 3, k % 3
                    T.op("act", lambda: nc.scalar.activation(out=ww[:, sw, c0:512], in_=ps[bt][:, c0:512],
                                                             func=AF.Exp),
                         reads=[ps_b[bt]], writes=[ww_b[sw]])

                def sbF(it):
                    h, g, kb, c0, k, idx, bo = it["h"], it["g"], it["kb"], it["c0"], it["k"], it["idx"], it["bo"]
                    j, po, t0, sw = h // 2, (h % 2) * 64, g * 512, k % 3
                    last = idx == it["n"] - 1
                    T.op("pe", lambda: nc.tensor.matmul(ps[bo][:, c0:512],
                                                        lhsT=vS[:, kb, j * 128:(j + 1) * 128],
                                                        rhs=ww[:, sw, c0:512], start=(idx == 0), stop=last,
                                                        skip_group_check=True),
                         reads=[vS_b[kb], ww_b[sw]], writes=[ps_b[bo]])
                    if last:
                        T.op("dve", lambda: nc.vector.tensor_copy(out=oT_sb[po:po + 64, j, t0:t0 + 512],
                                                                  in_=ps[bo][po:po + 64, :]),
                             reads=[ps_b[bo]], writes=[oTsb_b[j][g]])

                NI = len(items)
                for k in range(NI + 3):
                    if k < NI:
                        sbA(items[k])
                        sbB(items[k])
                    if 0 <= k - 1 < NI:
                        sbC(items[k - 1])
                        sbD(items[k - 1])
                    if 0 <= k - 2 < NI:
                        sbE(items[k - 2])
                    if 0 <= k - 3 < NI:
                        sbF(items[k - 3])
                T.barrier(skip=("pool",), keep=wkeep)
            if sq == 0:
                dump("oT_sb", oT_sb[:], sum(oTsb_b, []), [128, 4, S], BF16)

            with ExitStack() as esD:
                sa = lambda name, shape, dt: esD.enter_context(nc.sbuf_tensor("%s_%d" % (name, sq), list(shape), dt))
                qa = sa("qa", [128, 4, S], BF16)
                ka = sa("ka", [128, 4, S], BF16)
                vM = sa("vM", [128, 16, 4, 2, 128], BF16)
                TB = sa("TB", [128, 8, TBW], BF16)
                tb_b = T.buf("TB")
                T.dma("sp", TB[:].rearrange("p h j -> p (h j)"), tbd_d.ap(), tb_s, writes=[tb_b])
                pp = sa("pp", [128, 4, 512], BF16)
                rden = sa("rden", [128, 2, 512], F32)
                kbf = sa("kbf", [128, 4, 8], F32)
                kbT = sa("kbT", [128, 4, 8], BF16)
                gsb = sa("gsb", [128, 2, 4, 8], F32)
                m8 = sa("m8", [128, 2, 4, 8], F32)
                sel = sa("sel", [128, 2, 4, 8], F32)
                vM_b = T.bufs_n("vM", 16)
                pp_b = T.bufs_n("pp", 4)
                rden_b = T.bufs_n("rden", 2)
                kb_b = T.buf("kbar")
                gs_b = T.bufs_n("gs", 2)
                sel_b = T.bufs_n("selb", 2)
                mp_b = T.bufs_n("maskpad", 3)

                qa_b = [T.bufs_n("qa%d_" % hl, 4) for hl in range(4)]
                ka_b = [T.bufs_n("ka%d_" % hl, 4) for hl in range(4)]
                qm_b = T.bufs_n("qm", 16)
                kind_b = T.buf("kind")
                for hl in range(4):
                    T.dma("sp", ka[64:72, hl, :], kindd_d.ap(), kind_s, writes=[kind_b])
                T.op("dve", lambda: nc.vector.memset(qa[64:72, :, 0:1024], 0.0), writes=qm_b[0:8])
                for hh in range(2):
                    wq, wqb = wload(w_in_d, 3072, 0, 8, 1536, 512)
                    wk, wkb = wload(w_in_d, 3072, 0, 8, 2048, 512)
                    for (wblk, wb, dstt, dst_b2, scl) in ((wq, wqb, qa, qa_b, 0.125), (wk, wkb, ka, ka_b, None)):
                        for pr in range(2):
                            col0 = (hh * 4 + 2 * pr) * 64
                            for g in range(4):
                                bi = bank()
                                for c in range(8):
                                    T.op("pe", lambda c=c: nc.tensor.matmul(
                                        ps[bi][:, :], lhsT=wblk[:, c, col0:col0 + 128],
                                        rhs=hT[:, c, g * 512:(g + 1) * 512], start=(c == 0), stop=(c == 7)),
                                         reads=[wb] + hT_b[4 * g:4 * g + 4], writes=[ps_b[bi]], inc=(c == 7))
                                evac_copy("dve", dstt[0:64, 2 * pr, g * 512:(g + 1) * 512], ps[bi][0:64, :],
                                          [ps_b[bi]], [dst_b2[2 * pr][g]], scl)
                                evac_copy("act", dstt[0:64, 2 * pr + 1, g * 512:(g + 1) * 512], ps[bi][64:128, :],
                                          [ps_b[bi]], [dst_b2[2 * pr + 1][g]], scl)
                    if hh == 0:
                        wv, wvb = wload(w_in_d, 3072, 0, 8, 2560, 512)
                        T.op("dve", lambda: nc.vector.memset(vM[:, :, :, 0, 64:128], 1.0), writes=vM_b)
                        T.op("dve", lambda: nc.vector.memset(vM[:, :, :, 1, 0:64], 1.0), writes=vM_b)
                        proj_tm(wv, wvb, vM, vM_b, aug=True)
                    for hl in range(4):
                        T.op("dve", lambda hl=hl: nc.vector.tensor_reduce(
                            out=kbf[0:64, hl, :], in_=ka[0:64, hl, :].rearrange("p (n s) -> p n s", s=256),
                            axis=AX.X, op=ALU.add), reads=ka_b[hl], writes=[kb_b])
                    T.op("dve", lambda: nc.vector.tensor_scalar(out=kbT[0:64, :, :], in0=kbf[0:64, :, :],
                                                                scalar1=1.0 / 256, scalar2=None, op0=ALU.mult),
                         reads=[kb_b], writes=[kb_b])
                    def gate1(i):
                        c = i // 2
                        gsl = i % 2
                        bg = 7
                        for hl in range(4):
                            T.op("pe", lambda hl=hl: nc.tensor.matmul(ps[bg][:, hl * 8:(hl + 1) * 8],
                                                                      lhsT=qa[0:64, hl, i * 128:(i + 1) * 128],
                                                                      rhs=kbT[0:64, hl, :], start=True, stop=True),
                                 reads=[qa_b[hl][i // 4], kb_b], writes=[ps_b[bg]], inc=(hl == 3))
                        T.op("dve", lambda: nc.vector.tensor_copy(
                            out=gsb[:, gsl, :, :], in_=ps[bg][:, 0:32].rearrange("p (h n) -> p h n", h=4)),
                             reads=[ps_b[bg]], writes=[gs_b[gsl]])
                        if c < 8:
                            T.op("dve", lambda: nc.vector.memset(gsb[:, gsl, :, c:8], -1e30), writes=[gs_b[gsl]])
                        for hl in range(4):
                            T.op("dve", lambda hl=hl: nc.vector.max(out=m8[:, gsl, hl, :], in_=gsb[:, gsl, hl, :]),
                                 reads=[gs_b[gsl]], writes=[gs_b[gsl]])
                        for hl in range(4):
                            T.op("pool", lambda hl=hl: nc.gpsimd.tensor_scalar(
                                out=sel[:, gsl, hl, :], in0=gsb[:, gsl, hl, :], scalar1=m8[:, gsl, hl, 2:3],
                                scalar2=None, op0=ALU.is_ge), reads=[gs_b[gsl]], writes=[sel_b[gsl]])
                        ms = i % 3
                        T.op("pool", lambda: nc.gpsimd.tensor_scalar(
                            out=maskpad[:, ms, :, 64:72], in0=sel[:, gsl, :, :], scalar1=-1.0, scalar2=-NEG,
                            op0=ALU.add, op1=ALU.mult), reads=[sel_b[gsl]], writes=[mp_b[ms]])
                        T.op("pool", lambda: nc.gpsimd.memset(maskpad[:, ms, :, 64 + c:72], 0.0), writes=[mp_b[ms]])

                    def gate2(i):
                        bm = 7
                        ms = i % 3
                        for hl in range(4):
                            T.op("pe", lambda hl=hl: nc.tensor.matmul(ps[bm][0:72, hl * 128:(hl + 1) * 128],
                                                                      lhsT=maskpad[:, ms, hl, :], rhs=ident[:],
                                                                      start=True, stop=True),
                                 reads=[mp_b[ms]], writes=[ps_b[bm]], inc=(hl == 3))
                        T.op("act", lambda: nc.scalar.copy(
                            out=qa[64:72, :, i * 128:(i + 1) * 128],
                            in_=ps[bm][64:72, :].rearrange("p (h t) -> p h t", h=4)),
                             reads=[ps_b[bm]], writes=[qm_b[i]])

                    gsched = {}
                    g1s = {8: 12, 9: 20, 10: 28, 11: 36, 12: 48, 13: 58, 14: 68, 15: 78}
                    g2s = {8: 24, 9: 32, 10: 40, 11: 47, 12: 60, 13: 70, 14: 80, 15: 90}
                    for i in range(8, 16):
                        gsched.setdefault(g1s[i], []).append(lambda i=i: gate1(i))
                        gsched.setdefault(g2s[i], []).append(lambda i=i: gate2(i))
                    mits = []
                    g2 = 0
                    for m in range(4):
                        for hl in range(4):
                            bn = rbank("mbn", [0, 1])
                            nt = 4 * m + 4
                            for kb in range(nt):
                                mits.append(dict(hl=hl, m=m, kb=kb, ti=kb, nt=nt, bn=bn, k=len(mits), g2=g2))
                            g2 += 1

                    def mgeom(it):
                        m, kb = it["m"], it["kb"]
                        t0 = 512 * m
                        delta = t0 - 128 * kb
                        c0 = max(0, -delta)
                        return t0, delta, c0, delta + c0, 512 - c0, delta <= 128

                    def mbA(it):
                        hl, m, kb, k = it["hl"], it["m"], it["kb"], it["k"]
                        h = hh * 4 + hl
                        t0, delta, c0, j0, ncol, near = mgeom(it)
                        bl = 2 + k % 5
                        sl = k % 4
                        T.op("pe", lambda: nc.tensor.matmul(
                            ps[bl][:, c0:512], lhsT=ka[0:72, hl, kb * 128:(kb + 1) * 128],
                            rhs=qa[0:72, hl, t0 + c0:t0 + 512], start=True, stop=not near),
                             reads=[ka_b[hl][kb // 4], kind_b, qa_b[hl][m]] + qm_b[4 * m:4 * m + 4],
                             writes=[ps_b[bl]], inc=not near)
                        if near:
                            T.op("pe", lambda: nc.tensor.matmul(
                                ps[bl][:, c0:512], lhsT=ident[:], rhs=TB[:, h, j0:j0 + ncol],
                                start=False, stop=True), reads=[tb_b], writes=[ps_b[bl]])
                            T.op("act", lambda: nc.scalar.activation(
                                out=pp[:, sl, c0:512], in_=ps[bl][:, c0:512], func=AF.Exp),
                                 reads=[ps_b[bl]], writes=[pp_b[sl]])
                        else:
                            T.op("act", lambda: nc.scalar.activation(
                                out=pp[:, sl, c0:512], in_=ps[bl][:, c0:512], func=AF.Exp,
                                bias=b31[:, h:h + 1], scale=1.0),
                                 reads=[ps_b[bl]], writes=[pp_b[sl]])

                    def mbD(it):
                        for _ in range(NDUM_MB):
                            T.op("pe", lambda: nc.tensor.matmul(ps[7][:, 0:256], lhsT=ident[:], rhs=hT[:, 0, 0:256],
                                                                start=True, stop=True),
                                 writes=[ps_b[7]], inc=False)

                    def mbC(it):
                        hl, m, kb, k, bn = it["hl"], it["m"], it["kb"], it["k"], it["bn"]
                        h = hh * 4 + hl
                        j = h // 2
                        po = (h % 2) * 64
                        t0, delta, c0, j0, ncol, near = mgeom(it)
                        sl = k % 4
                        first = it["ti"] == 0
                        lastt = it["ti"] == it["nt"] - 1
                        T.op("pe", lambda: nc.tensor.matmul(
                            ps[bn][:, c0:512], lhsT=vM[:, kb, j, h % 2, :],
                            rhs=pp[:, sl, c0:512], start=first, stop=lastt, skip_group_check=True),
                             reads=[vM_b[kb], pp_b[sl]], writes=[ps_b[bn]], inc=True)
                        if lastt:
                            rs = it["g2"] % 2
                            T.op("act", lambda: nc.scalar.copy(out=rden[po:po + 64, rs, :],
                                                               in_=ps[bn][64 - po:128 - po, :]),
                                 reads=[ps_b[bn]], writes=[rden_b[rs]])
                            T.op("dve", lambda: nc.vector.reciprocal(out=rden[po:po + 64, rs, :],
                                                                     in_=rden[po:po + 64, rs, :]),
                                 reads=[rden_b[rs]], writes=[rden_b[rs]])
                            T.op("dve", lambda: nc.vector.tensor_tensor(
                                out=oT_mb[po:po + 64, j, t0:t0 + 512], in0=ps[bn][po:po + 64, :],
                                in1=rden[po:po + 64, rs, :], op=ALU.mult),
                                 reads=[ps_b[bn], rden_b[rs]], writes=[oTmb_b[j][m]])

                    NM = len(mits)
                    for k in range(NM + 2):
                        for fn_ in gsched.get(k, []):
                            fn_()
                        if k < NM:
                            mbA(mits[k])
                            mbD(mits[k])
                        if 0 <= k - 2 < NM:
                            mbC(mits[k - 2])
                T.barrier(skip=("pool",), keep=wkeep)
            if sq == 0:
                dump("oT_mb", oT_mb[:], sum(oTmb_b, []), [128, 4, S], BF16)

            with ExitStack() as esE:
                sa = lambda name, shape, dt: esE.enter_context(nc.sbuf_tensor("%s_%d" % (name, sq), list(shape), dt))
                xs = sa("xsE", [128, 4, D], F32)
                gfin = sa("gfin", [128, D], F32)
                gfin_b = T.buf("gfin")
                T.dma("sp", gfin[:], bass.AP(fin_d, 0, [[0, 128], [1, D]]), gfin_s, writes=[gfin_b])
                xn = sa("xnE", [128, 3, D], BF16)
                junk = sa("junkE", [128, D], F32)
                stt = sa("sttE", [128, 8], F32)
                actT = sa("actT", [128, 8, 512], BF16)
                aT = sa("aT", [128, NFC, 512], BF16)
                fscr = sa("fscr", [128, 4, 512], F32)
                pT = sa("pT", [128, 2, 512], BF16)
                pt = sa("pt", [128, 4, PLE], F32)
                pb = sa("pb", [128, 4, PLE], BF16)
                xs_b = T.bufs_n("xsE", 4)
                xn_b = T.bufs_n("xnE", 3)
                junk_b = T.buf("junkE")
                st_b = T.bufs_n("stE", 4)
                actT_b = T.bufs_n("actT", 4)
                aT_b = T.bufs_n("aT", NFC)
                fs_b = T.bufs_n("fscr", 4)
                pT_b = T.bufs_n("pT", 4)
                pt_b = T.bufs_n("pt", 4)
                pb_b = T.bufs_n("pb", 4)
                fsi = [0]

                def fs():
                    i = fsi[0]
                    fsi[0] = (i + 1) % 4
                    return i

                for g in range(4):
                    tok = slice(g * 512, (g + 1) * 512)
                    for i in range(4):
                        r0 = g * 512 + i * 128
                        T.dma("sp", xs[:, i, :], x_d.ap()[sq, r0:r0 + 128, :], xs_s[i], writes=[xs_b[i]])
                    for i in range(4):
                        r0 = g * 512 + i * 128
                        T.dma("sp", pt[:, i, :], p_d.ap()[sq, r0:r0 + 128, :], pt_s[i], writes=[pt_b[i]])
                        T.op("dve", lambda: nc.vector.tensor_copy(out=pb[:, i, :], in_=pt[:, i, :]),
                             reads=[pt_b[i]], writes=[pb_b[i]])
                    for hf in range(2):
                        wgs, wgsb = wload(w_gate_d, 2048, 0, 8, hf * 512, 512)
                        wgm, wgmb = wload(w_gate_d, 2048, 0, 8, 1024 + hf * 512, 512)
                        if hf == 0:
                            wbs, wbsb = wload(w_bsb_d, D, 0, 4, 0, 1024)
                            wbm, wbmb = wload(w_bmb_d, D, 0, 4, 0, 1024)
                        for dq in range(4):
                            dc = hf * 4 + dq
                            bys, bym, bgs, bgm = bank(), bank(), bank(), bank()
                            for (bb, wblk, wb) in ((bgs, wgs, wgsb), (bgm, wgm, wgmb)):
                                for cc in range(8):
                                    T.op("pe", lambda cc=cc, bb=bb, wblk=wblk: nc.tensor.matmul(
                                        ps[bb][:, :], lhsT=wblk[:, cc, dq * 128:(dq + 1) * 128], rhs=hT[:, cc, tok],
                                        start=(cc == 0), stop=(cc == 7)),
                                         reads=[wb] + hT_b[4 * g:4 * g + 4], writes=[ps_b[bb]], inc=(cc == 7))
                            for (bb, wblk, wb, oT, oTb) in ((bys, wbs, wbsb, oT_sb, oTsb_b),
                                                            (bym, wbm, wbmb, oT_mb, oTmb_b)):
                                for jj in range(4):
                                    T.op("pe", lambda jj=jj, bb=bb, wblk=wblk, oT=oT: nc.tensor.matmul(
                                        ps[bb][:, :], lhsT=wblk[:, jj, dc * 128:(dc + 1) * 128], rhs=oT[:, jj, tok],
                                        start=(jj == 0), stop=(jj == 3)),
                                         reads=[wb, oTb[jj][g]], writes=[ps_b[bb]], inc=(jj == 3))
                            f1, f2 = fs(), fs()
                            T.op("act", lambda: nc.scalar.activation(out=fscr[:, f1, :], in_=ps[bgs][:, :],
                                                                     func=AF.Sigmoid, bias=bgT[:, dc:dc + 1],
                                                                     scale=1.0),
                                 reads=[ps_b[bgs]], writes=[fs_b[f1]])
                            T.op("act", lambda: nc.scalar.activation(out=fscr[:, f2, :], in_=ps[bgm][:, :],
                                                                     func=AF.Sigmoid, bias=bgT[:, 8 + dc:9 + dc],
                                                                     scale=1.0),
                                 reads=[ps_b[bgm]], writes=[fs_b[f2]])
                            T.op("dve", lambda: nc.vector.tensor_tensor(out=fscr[:, f1, :], in0=fscr[:, f1, :],
                                                                        in1=ps[bys][:, :], op=ALU.mult),
                                 reads=[fs_b[f1], ps_b[bys]], writes=[fs_b[f1]])
                            T.op("dve", lambda: nc.vector.tensor_tensor(out=fscr[:, f2, :], in0=fscr[:, f2, :],
                                                                        in1=ps[bym][:, :], op=ALU.mult),
                                 reads=[fs_b[f2], ps_b[bym]], writes=[fs_b[f2]])
                            T.op("dve", lambda: nc.vector.tensor_tensor(out=actT[:, dc, :], in0=fscr[:, f1, :],
                                                                        in1=fscr[:, f2, :], op=ALU.add),
                                 reads=[fs_b[f1], fs_b[f2]], writes=actT_b)
                    wo2 = [wload(w_out_d, D, 0, 8, cb * 512, 512) for cb in range(2)]
                    pend = None
                    for i in range(4):
                        for cb in range(2):
                            wo, wob = wo2[cb]
                            bi = bank()
                            for dc in range(8):
                                T.op("pe", lambda dc=dc: nc.tensor.matmul(
                                    ps[bi][:, :], lhsT=actT[:, dc, i * 128:(i + 1) * 128], rhs=wo[:, dc, :],
                                    start=(dc == 0), stop=(dc == 7)),
                                     reads=[wob, actT_b[i]], writes=[ps_b[bi]], inc=(dc == 7))
                            T.op("dve", lambda: nc.vector.tensor_tensor(
                                out=xs[:, i, cb * 512:(cb + 1) * 512], in0=xs[:, i, cb * 512:(cb + 1) * 512],
                                in1=ps[bi][:, :], op=ALU.add),
                                 reads=[xs_b[i], ps_b[bi]], writes=[xs_b[i]])
                        norm_chain(xs[:, i, :], xs_b[i], stt[:, i:i + 1], stt[:, 4 + i:5 + i], st_b[i],
                                   junk[:], junk_b, xn[:, i % 3, :], xn_b[i % 3])
                        if i >= 2:
                            norm_tr(xn[:, (i - 2) % 3, :], xn_b[(i - 2) % 3], gffnT,
                                    actT[:, :, (i - 2) * 128:(i - 1) * 128], actT_b[i - 2])
                    for pend in (2, 3):
                        norm_tr(xn[:, pend % 3, :], xn_b[pend % 3], gffnT,
                                actT[:, :, pend * 128:(pend + 1) * 128], actT_b[pend])
                    if sq == 0 and g == 0:
                        dump("x1", xs[:], xs_b, [128, 4, D], F32)
                    for fb in range(6):
                        ncol = 512 if fb < 5 else 256
                        wg, wgb = wload(w_fg_d, FF, 0, 8, fb * 512, ncol)
                        wu, wub = wload(w_fu_d, FF, 0, 8, fb * 512, ncol)
                        for fc in range(ncol // 128):
                            f = fb * 4 + fc
                            bgt, but = bank(), bank()
                            for (bb, wblk, wb) in ((bgt, wg, wgb), (but, wu, wub)):
                                for cc in range(8):
                                    T.op("pe", lambda cc=cc, bb=bb, wblk=wblk: nc.tensor.matmul(
                                        ps[bb][:, :], lhsT=wblk[:, cc, fc * 128:(fc + 1) * 128], rhs=actT[:, cc, :],
                                        start=(cc == 0), stop=(cc == 7)),
                                         reads=[wb] + actT_b, writes=[ps_b[bb]], inc=(cc == 7))
                            f1 = fs()
                            T.op("act", lambda: nc.scalar.activation(out=fscr[:, f1, :], in_=ps[bgt][:, :],
                                                                     func=AF.Silu),
                                 reads=[ps_b[bgt]], writes=[fs_b[f1]])
                            T.op("dve", lambda: nc.vector.tensor_tensor(out=aT[:, f, :], in0=fscr[:, f1, :],
                                                                        in1=ps[but][:, :], op=ALU.mult),
                                 reads=[fs_b[f1], ps_b[but]], writes=[aT_b[f]])
                    for i in range(4):
                        s2 = i
                        bi = bank()
                        pst = ps[bi][:].bitcast(BF16)
                        for cc in range(2):
                            T.op("pe", lambda cc=cc: nc.tensor.transpose(out=pst[:, cc * 128:(cc + 1) * 128],
                                                                         in_=pb[:, s2, cc * 128:(cc + 1) * 128],
                                                                         identity=ident[:]),
                                 reads=[pb_b[s2]], writes=[ps_b[bi]], inc=(cc == 1))
                        T.op("act", lambda: nc.scalar.copy(
                            out=pT[:, :, i * 128:(i + 1) * 128],
                            in_=pst[:, 0:256].rearrange("p (c t) -> p c t", c=2)),
                             reads=[ps_b[bi]], writes=[pT_b[i]])
                    pend = None
                    for cb in range(2):
                        blks = []
                        for ks in range(3):
                            nk = 8 if ks < 2 else 6
                            blks.append(wload(w_fd_d, D, ks * 8, nk, cb * 512, 512))
                        for i in range(4):
                            bi = bank()
                            for f in range(NFC):
                                wblk, wb = blks[f // 8]
                                T.op("pe", lambda f=f, wblk=wblk: nc.tensor.matmul(
                                    ps[bi][:, :], lhsT=aT[:, f, i * 128:(i + 1) * 128], rhs=wblk[:, f % 8, :],
                                    start=(f == 0), stop=(f == NFC - 1)),
                                     reads=[wb, aT_b[f]], writes=[ps_b[bi]], inc=(f == NFC - 1))
                            T.op("dve", lambda: nc.vector.tensor_tensor(
                                out=xs[:, i, cb * 512:(cb + 1) * 512], in0=xs[:, i, cb * 512:(cb + 1) * 512],
                                in1=ps[bi][:, :], op=ALU.add),
                                 reads=[xs_b[i], ps_b[bi]], writes=[xs_b[i]])
                            if cb == 1:
                                norm_chain(xs[:, i, :], xs_b[i], stt[:, i:i + 1], stt[:, 4 + i:5 + i], st_b[i],
                                           junk[:], junk_b, xn[:, i % 3, :], xn_b[i % 3])
                                if pend is not None:
                                    norm_tr(xn[:, pend % 3, :], xn_b[pend % 3], gpleT,
                                            actT[:, :, pend * 128:(pend + 1) * 128], actT_b[pend])
                                pend = i
                    if sq == 0 and g == 0:
                        dump("x2", xs[:], xs_b, [128, 4, D], F32)
                    wpp, wppb = wload(w_pp_d, D, 0, 2, 0, 1024)
                    wpg2 = [wload(w_pg_d, D, 0, 8, cb * 512, 512) for cb in range(2)]
                    for i in range(4):
                        if i == pend:
                            norm_tr(xn[:, pend % 3, :], xn_b[pend % 3], gpleT,
                                    actT[:, :, pend * 128:(pend + 1) * 128], actT_b[pend])
                        for cb in range(2):
                            wpg, wpgb = wpg2[cb]
                            bgt, bpp = bank(), bank()
                            for cc in range(8):
                                T.op("pe", lambda cc=cc: nc.tensor.matmul(
                                    ps[bgt][:, :], lhsT=actT[:, cc, i * 128:(i + 1) * 128], rhs=wpg[:, cc, :],
                                    start=(cc == 0), stop=(cc == 7)),
                                     reads=[wpgb, actT_b[i]], writes=[ps_b[bgt]], inc=(cc == 7))
                            for cc in range(2):
                                T.op("pe", lambda cc=cc: nc.tensor.matmul(
                                    ps[bpp][:, :], lhsT=pT[:, cc, i * 128:(i + 1) * 128],
                                    rhs=wpp[:, cc, cb * 512:(cb + 1) * 512], start=(cc == 0), stop=(cc == 1)),
                                     reads=[wppb, pT_b[i]], writes=[ps_b[bpp]], inc=(cc == 1))
                            f1 = fs()
                            T.op("act", lambda: nc.scalar.activation(out=fscr[:, f1, :], in_=ps[bgt][:, :],
                                                                     func=AF.Sigmoid),
                                 reads=[ps_b[bgt]], writes=[fs_b[f1]])
                            T.op("dve", lambda: nc.vector.tensor_tensor(out=fscr[:, f1, :], in0=fscr[:, f1, :],
                                                                        in1=ps[bpp][:, :], op=ALU.mult),
                                 reads=[fs_b[f1], ps_b[bpp]], writes=[fs_b[f1]])
                            T.op("dve", lambda: nc.vector.tensor_tensor(
                                out=xs[:, i, cb * 512:(cb + 1) * 512], in0=xs[:, i, cb * 512:(cb + 1) * 512],
                                in1=fscr[:, f1, :], op=ALU.add),
                                 reads=[xs_b[i], fs_b[f1]], writes=[xs_b[i]])
                    for i in range(4):
                        norm_rstd(xs[:, i, :], xs_b[i], stt[:, i:i + 1], stt[:, 4 + i:5 + i], st_b[i],
                                  junk[:], junk_b)
                        T.op("dve", lambda: nc.vector.scalar_tensor_tensor(
                            out=xs[:, i, :], in0=xs[:, i, :], scalar=stt[:, 4 + i:5 + i], in1=gfin[:],
                            op0=ALU.mult, op1=ALU.mult),
                             reads=[xs_b[i], st_b[i], gfin_b], writes=[xs_b[i]])
                        r0 = g * 512 + i * 128
                        T.dma("sp", y_d.ap()[sq, r0:r0 + 128, :], xs[:, i, :], out_s[i], reads=[xs_b[i]])
                T.barrier(skip=("pool",), keep=wkeep)
        T.barrier()
    return nc, dbg_t


_W_NAMES = ["ln_mix_g", "w_in", "w_gate", "b_gate", "w_branch_sb", "w_branch_moba", "w_out",
            "ln_ffn_g", "w_ffn_gate", "w_ffn_up", "w_ffn_down", "ln_ple_g", "w_ple_gate", "w_ple_proj"]


def make_in_maps(inputs, ncores, nseq):
    shared = {k: np.ascontiguousarray(np.asarray(inputs[k], np.float32)[0]) for k in _W_NAMES}
    shared["rel_bias"] = np.ascontiguousarray(np.asarray(inputs["rel_bias"], np.float32))
    shared["final_g"] = np.ascontiguousarray(np.asarray(inputs["final_g"], np.float32))
    shared.update(_consts())
    x = np.asarray(inputs["x"], np.float32)
    p = np.asarray(inputs["p"], np.float32)[0]
    maps = []
    for c in range(ncores):
        m = dict(shared)
        m["x"] = np.ascontiguousarray(x[c * nseq:(c + 1) * nseq])
        m["p"] = np.ascontiguousarray(p[c * nseq:(c + 1) * nseq])
        maps.append(m)
    return maps


def kernel(**inputs):
    nseq = BATCH // NCORES
    nc, _ = build(nseq)
    maps = make_in_maps(inputs, NCORES, nseq)
    res = run_bass_kernel_spmd(nc, maps, core_ids=list(range(NCORES)))
    out = np.concatenate([np.asarray(r["y"]) for r in res.results], axis=0)
    return out.astype(np.float32)
```

```python
import math
from contextlib import ExitStack

import numpy as np
import concourse.bass as bass
import concourse.mybir as mybir
from concourse.bass_utils import run_bass_kernel_spmd

F32 = mybir.dt.float32
BF16 = mybir.dt.bfloat16
AF = mybir.ActivationFunctionType
ALU = mybir.AluOpType
AX = mybir.AxisListType

D = 1024
S = 2048
BATCH = 32
NCORES = 8
HD = 64
NH = 8
FF = 2816
NFC = FF // 128
PLE = 256
NEG = -30000.0
EPS = 1e-6
LV = 768
TBW = 640
NSLOT = 6
NDUM_SB = 0
NDUM_MB = 0


class Buf:
    __slots__ = ("name", "w", "r")

    def __init__(self, name):
        self.name = name
        self.w = None
        self.r = {}


class DSem:
    def __init__(self, handle, key):
        self.h = handle
        self.key = key
        self.count = 0


class Eng:
    def __init__(self, name, handle, sem):
        self.name = name
        self.h = handle
        self.sem = sem
        self.key = "E_" + name
        self.count = 0
        self.known = {}


class Tracker:
    def __init__(self, nc, es):
        self.nc = nc
        self.es = es
        self.E = {}
        for name, h in (("pe", nc.tensor), ("act", nc.scalar), ("dve", nc.vector),
                        ("pool", nc.gpsimd), ("sp", nc.sync)):
            self.E[name] = Eng(name, h, es.enter_context(nc.semaphore("sem_" + name)))
        self.dsems = []
        self.bufs = []

    def buf(self, name):
        b = Buf(name)
        self.bufs.append(b)
        return b

    def bufs_n(self, name, n):
        return [self.buf("%s%d" % (name, i)) for i in range(n)]

    def dsem(self, name):
        d = DSem(self.es.enter_context(self.nc.semaphore("d_" + name)), "D_" + name)
        self.dsems.append(d)
        return d

    def _waits(self, E, reads, writes):
        need = {}

        def req(ev, same_ok):
            if ev is None:
                return
            k, sem, val = ev
            if k == E.key and E.name == "pe":
                return
            if need.get(k, (None, 0))[1] < val:
                need[k] = (sem, val)

        for b in reads:
            req(b.w, False)
        for b in writes:
            req(b.w, True)
            for ev in b.r.values():
                req(ev, True)
        for k, (sem, val) in need.items():
            if E.known.get(k, 0) >= val:
                continue
            E.h.wait_ge(sem, val)
            E.known[k] = val

    def _record(self, ev, reads, writes):
        k = ev[0]
        for b in reads:
            old = b.r.get(k)
            if old is None or old[2] < ev[2]:
                b.r[k] = ev
        for b in writes:
            b.w = ev
            b.r = {}

    def op(self, eng, fn, reads=(), writes=(), inc=True):
        E = self.E[eng]
        self._waits(E, reads, writes)
        ins = fn()
        if inc:
            E.count += 1
            ins.then_inc(E.sem, 1)
            ev = (E.key, E.sem, E.count)
        else:
            ev = (E.key, E.sem, E.count + 1)
        self._record(ev, reads, writes)
        return ins

    def dma(self, q, out, in_, sem, reads=(), writes=(), **kw):
        E = self.E[q]
        self._waits(E, reads, writes)
        ins = E.h.dma_start(out=out, in_=in_, **kw)
        sem.count += 16
        ins.then_inc(sem.h, 16)
        ev = (sem.key, sem.h, sem.count)
        self._record(ev, reads, writes)
        return ins

    def barrier(self, skip=(), keep=()):
        evs = [(e.key, e.sem, e.count) for e in self.E.values() if e.count > 0]
        evs += [(d.key, d.h, d.count) for d in self.dsems if d.count > 0]
        for E in self.E.values():
            if E.name in skip:
                continue
            for k, sem, val in evs:
                if k == E.key:
                    continue
                if E.known.get(k, 0) >= val:
                    continue
                E.h.wait_ge(sem, val)
                E.known[k] = val
        keep_ids = set(id(b) for b in keep)
        for b in self.bufs:
            if id(b) in keep_ids:
                continue
            b.w = None
            b.r = {}


def _bucket_table():
    d = np.arange(LV) - 127
    n = np.maximum(d, 0)
    nf = np.maximum(n, 1).astype(np.float32)
    large = 16 + (np.log(nf / np.float32(16)).astype(np.float32) / np.float32(math.log(8.0))
                  * np.float32(16)).astype(np.int32)
    large = np.minimum(large, 31)
    bk = np.where(n < 16, n, large)
    oh = np.zeros((33, LV), np.float32)
    for i in range(LV):
        if d[i] < 0:
            oh[32, i] = 1.0
        else:
            oh[bk[i], i] = 1.0
    return oh


def _consts():
    p = np.arange(128)[:, None]
    j = np.arange(128)[None, :]
    c = {}
    c["c_ident"] = np.eye(128, dtype=np.float32)
    c["c_negui"] = np.where(p >= j, -1.0, 0.0).astype(np.float32)
    c["c_negones"] = -np.ones((128, 128), np.float32)
    c["c_ones"] = np.ones((128, 128), np.float32)
    c["c_mtri"] = np.where(j <= p, NEG, 0.0).astype(np.float32)
    c["c_oh"] = _bucket_table()
    kind = np.zeros((8, S), np.float32)
    for n in range(8):
        kind[n, n * 256:(n + 1) * 256] = 1.0
    c["c_kind"] = kind
    return c


def build(nseq, dbg=None):
    dbg = dbg or set()
    nc = bass.Bass("TRN2", target_bir_lowering=False)

    def din(name, shape):
        return nc.dram_tensor(name, list(shape), F32, kind="ExternalInput")

    x_d = din("x", [nseq, S, D])
    p_d = din("p", [nseq, S, PLE])
    ln_mix_d = din("ln_mix_g", [D])
    w_in_d = din("w_in", [D, 3072])
    w_gate_d = din("w_gate", [D, 2048])
    b_gate_d = din("b_gate", [2048])
    w_bsb_d = din("w_branch_sb", [512, D])
    w_bmb_d = din("w_branch_moba", [512, D])
    w_out_d = din("w_out", [D, D])
    relb_d = din("rel_bias", [32, 8])
    ln_ffn_d = din("ln_ffn_g", [D])
    w_fg_d = din("w_ffn_gate", [D, FF])
    w_fu_d = din("w_ffn_up", [D, FF])
    w_fd_d = din("w_ffn_down", [FF, D])
    ln_ple_d = din("ln_ple_g", [D])
    w_pg_d = din("w_ple_gate", [D, D])
    w_pp_d = din("w_ple_proj", [PLE, D])
    fin_d = din("final_g", [D])
    c_ident_d = din("c_ident", [128, 128])
    c_negui_d = din("c_negui", [128, 128])
    c_negones_d = din("c_negones", [128, 128])
    c_ones_d = din("c_ones", [128, 128])
    c_mtri_d = din("c_mtri", [128, 128])
    c_oh_d = din("c_oh", [33, LV])
    c_kind_d = din("c_kind", [8, S])
    y_d = nc.dram_tensor("y", [nseq, S, D], F32, kind="ExternalOutput")
    vd_d = nc.dram_tensor("vd_scr", [8 * LV], F32, kind="Internal")
    fd_d = nc.dram_tensor("fd_scr", [8 * 128 * LV], F32, kind="Internal")
    tbd_d = nc.dram_tensor("tbd_scr", [128, 8 * TBW], BF16, kind="Internal")
    kindd_d = nc.dram_tensor("kindd_scr", [8, S], BF16, kind="Internal")
    dbg_t = {}

    with ExitStack() as es:
        T = Tracker(nc, es)
        sb = lambda name, shape, dt: es.enter_context(nc.sbuf_tensor(name, list(shape), dt))

        ident = sb("ident", [128, 128], BF16)
        negui = sb("negui", [128, 128], BF16)
        negones = sb("negones", [128, 128], BF16)
        ones_b = sb("ones_b", [128, 128], BF16)
        mtri = sb("mtri", [128, 128], BF16)
        gmixT = sb("gmixT", [128, 8], F32)
        gffnT = sb("gffnT", [128, 8], F32)
        gpleT = sb("gpleT", [128, 8], F32)
        bgT = sb("bgT", [128, 16], F32)
        b31 = sb("b31", [128, 8], F32)
        maskpad = sb("maskpad", [128, 3, 4, 72], BF16)
        hT = sb("hT", [128, 8, S], BF16)
        oT_sb = sb("oT_sb", [128, 4, S], BF16)
        oT_mb = sb("oT_mb", [128, 4, S], BF16)
        wp = sb("wp", [128, NSLOT, 4096], BF16)
        ps = [es.enter_context(nc.psum_tensor("ps%d" % i, [128, 512], F32)) for i in range(8)]

        ps_b = T.bufs_n("ps", 8)
        wp_b = T.bufs_n("wp", NSLOT)
        wp_s = [T.dsem("wp%d" % i) for i in range(NSLOT)]
        hT_b = T.bufs_n("hT", 16)
        oTsb_b = [T.bufs_n("oTsb%d_" % j, 4) for j in range(4)]
        oTmb_b = [T.bufs_n("oTmb%d_" % j, 4) for j in range(4)]
        cst_b = T.buf("consts")
        cst_s = T.dsem("consts")
        cst2_s = T.dsem("consts2")
        out_s = [T.dsem("out%d" % i) for i in range(4)]
        xs_s = [T.dsem("xs%d" % i) for i in range(4)]
        pt_s = [T.dsem("pt%d" % i) for i in range(4)]
        misc_s = T.dsem("misc")
        tb_s = T.dsem("tbload")
        kind_s = T.dsem("kind")
        gfin_s = T.dsem("gfin")
        dbg_s = T.dsem("dbg")

        state = {"bank": 0, "wslot": 0}

        def bank():
            i = state["bank"]
            state["bank"] = (i + 1) % 8
            return i

        rolec = {}

        def rbank(role, banks):
            i = rolec.get(role, 0)
            rolec[role] = i + 1
            return banks[i % len(banks)]

        def dump(name, ap, bufs, shape, dt=F32):
            if name not in dbg:
                return
            t = nc.dram_tensor("dbg_" + name, list(shape), dt, kind="ExternalOutput")
            dbg_t[name] = t
            T.dma("sp", t.ap(), ap, dbg_s, reads=bufs)

        wcache = {}
        wkeep = list(wp_b)

        def wload(wd, ncols_total, row_chunk0, nk, col0, ncols):
            s = state["wslot"]
            state["wslot"] = (s + 1) % NSLOT
            dst = wp[:, s, 0:nk * ncols].rearrange("p (k n) -> p k n", k=nk)
            key = (wd.name, row_chunk0, nk, col0, ncols)
            if key not in wcache:
                src = bass.AP(wd, row_chunk0 * 128 * ncols_total + col0,
                              [[ncols_total, 128], [128 * ncols_total, nk], [1, ncols]])
                T.dma("pool", dst, src, wp_s[s], writes=[wp_b[s]])
                n = len(wcache)
                scr = nc.dram_tensor("wscr%d" % n, [128, nk * ncols], BF16, kind="Internal")
                sb_ = T.buf("wscr%d" % n)
                wkeep.append(sb_)
                ss_ = T.dsem("wscr%d" % n)
                T.dma("sp", scr.ap(), wp[:, s, 0:nk * ncols], ss_, reads=[wp_b[s]], writes=[sb_])
                wcache[key] = (scr, sb_)
            else:
                scr, sb_ = wcache[key]
                T.dma("pool", wp[:, s, 0:nk * ncols], scr.ap(), wp_s[s], reads=[sb_], writes=[wp_b[s]])
            return dst, wp_b[s]

        def cload(dst, src_d):
            T.dma("pool", dst, src_d, cst_s, writes=[cst_b])

        cload(ident[:], c_ident_d.ap())
        cload(negui[:], c_negui_d.ap())
        cload(negones[:], c_negones_d.ap())
        cload(ones_b[:], c_ones_d.ap())
        cload(mtri[:], c_mtri_d.ap())
        for gt, gd in ((gmixT, ln_mix_d), (gffnT, ln_ffn_d), (gpleT, ln_ple_d)):
            T.dma("sp", gt[:], bass.AP(gd, 0, [[1, 128], [128, 8]]), cst2_s, writes=[cst_b],
                  allow_slow_non_contiguous=True)
        T.dma("sp", bgT[:], bass.AP(b_gate_d, 0, [[1, 128], [128, 16]]), cst2_s, writes=[cst_b],
              allow_slow_non_contiguous=True)
        T.dma("sp", b31[:], bass.AP(relb_d, 31 * 8, [[0, 128], [1, 8]]), cst2_s, writes=[cst_b])
        T.op("dve", lambda: nc.vector.memset(maskpad[:], 0.0), writes=[cst_b])
        with ExitStack() as es0:
            rbx = es0.enter_context(nc.sbuf_tensor("rbx", [33, 8], F32))
            ohs = es0.enter_context(nc.sbuf_tensor("ohs", [33, LV], F32))
            vecsb = es0.enter_context(nc.sbuf_tensor("vecsb", [8, LV], F32))
            TB = es0.enter_context(nc.sbuf_tensor("TBtmp", [128, 8, TBW], BF16))
            kindt = es0.enter_context(nc.sbuf_tensor("kindtmp", [8, S], BF16))
            T.dma("pool", kindt[:], c_kind_d.ap(), cst_s, writes=[cst_b])
            T.op("dve", lambda: nc.vector.memset(rbx[32:33, :], NEG), writes=[cst_b])
            T.dma("sp", rbx[0:32, :], relb_d.ap(), cst2_s, writes=[cst_b])
            T.dma("sp", ohs[:], c_oh_d.ap(), cst2_s, writes=[cst_b])
            T.barrier()
            T.op("pe", lambda: nc.tensor.matmul(ps[0][0:8, 0:512], lhsT=rbx[0:33, :], rhs=ohs[0:33, 0:512],
                                                start=True, stop=True), writes=[ps_b[0]])
            T.op("pe", lambda: nc.tensor.matmul(ps[1][0:8, 0:LV - 512], lhsT=rbx[0:33, :], rhs=ohs[0:33, 512:LV],
                                                start=True, stop=True), writes=[ps_b[1]])
            T.op("act", lambda: nc.scalar.copy(out=vecsb[:, 0:512], in_=ps[0][0:8, 0:512]),
                 reads=[ps_b[0]], writes=[cst_b])
            T.op("act", lambda: nc.scalar.copy(out=vecsb[:, 512:LV], in_=ps[1][0:8, 0:LV - 512]),
                 reads=[ps_b[1]], writes=[cst_b])
            T.barrier()
            T.dma("sp", bass.AP(vd_d, 0, [[LV, 8], [1, LV]]), vecsb[:], cst2_s, reads=[cst_b])
            T.barrier()
            T.dma("sp", bass.AP(fd_d, 0, [[128 * LV, 8], [LV, 128], [1, LV]]),
                  bass.AP(vd_d, 0, [[LV, 8], [0, 128], [1, LV]]), cst2_s)
            T.barrier()
            for h in range(8):
                T.dma("pool", TB[:, h, :], bass.AP(fd_d, h * 128 * LV + 127, [[LV - 1, 128], [1, TBW]]),
                      cst_s, writes=[cst_b])
            T.barrier()
            T.dma("sp", tbd_d.ap(), TB[:].rearrange("p h j -> p (h j)"), cst2_s, reads=[cst_b])
            T.dma("sp", kindd_d.ap(), kindt[:], cst2_s, reads=[cst_b])
            T.barrier()

        def evac_copy(which, out, in_, reads, writes, scale=None):
            if which == "act":
                if scale is None:
                    T.op("act", lambda: nc.scalar.copy(out=out, in_=in_), reads=reads, writes=writes)
                else:
                    T.op("act", lambda: nc.scalar.activation(out=out, in_=in_, func=AF.Copy, scale=scale),
                         reads=reads, writes=writes)
            else:
                if scale is None:
                    T.op("dve", lambda: nc.vector.tensor_copy(out=out, in_=in_), reads=reads, writes=writes)
                else:
                    T.op("dve", lambda: nc.vector.tensor_scalar(out=out, in0=in_, scalar1=scale, scalar2=None,
                                                                op0=ALU.mult), reads=reads, writes=writes)

        for sq in range(nseq):
            def norm_rstd(xt, xt_b, ss, rstd, st_b, junk, junk_b):
                T.op("act", lambda: nc.scalar.activation(out=junk, in_=xt, func=AF.Square, accum_out=ss),
                     reads=[xt_b], writes=[junk_b, st_b])
                T.op("dve", lambda: nc.vector.tensor_scalar(out=rstd, in0=ss, scalar1=1.0 / D, scalar2=EPS,
                                                            op0=ALU.mult, op1=ALU.add),
                     reads=[st_b], writes=[st_b])
                T.op("act", lambda: nc.scalar.activation(out=rstd, in_=rstd, func=AF.Sqrt),
                     reads=[st_b], writes=[st_b])
                T.op("dve", lambda: nc.vector.reciprocal(out=rstd, in_=rstd),
                     reads=[st_b], writes=[st_b])

            def norm_chain(xt, xt_b, ss, rstd, st_b, junk, junk_b, xn, xn_b):
                norm_rstd(xt, xt_b, ss, rstd, st_b, junk, junk_b)
                T.op("dve", lambda: nc.vector.tensor_scalar(out=xn, in0=xt, scalar1=rstd, scalar2=None,
                                                            op0=ALU.mult),
                     reads=[xt_b, st_b], writes=[xn_b])

            def norm_tr(xn, xn_b, gT, dst, dst_b):
                bi = bank()
                pst = ps[bi][:].bitcast(BF16)
                for c in range(8):
                    T.op("pe", lambda c=c: nc.tensor.transpose(out=pst[:, c * 128:(c + 1) * 128],
                                                               in_=xn[:, c * 128:(c + 1) * 128],
                                                               identity=ident[:]),
                         reads=[xn_b], writes=[ps_b[bi]], inc=(c == 7))
                T.op("dve", lambda: nc.vector.tensor_tensor(
                    out=dst, in0=pst[:, 0:1024].rearrange("p (c t) -> p c t", c=8),
                    in1=gT[:, :].unsqueeze(2).broadcast_to([128, 8, 128]), op=ALU.mult),
                     reads=[ps_b[bi]], writes=[dst_b])

            def norm_transpose(xt, xt_b, ss, rstd, st_b, junk, junk_b, xn, xn_b, gT, dst, dst_b):
                norm_chain(xt, xt_b, ss, rstd, st_b, junk, junk_b, xn, xn_b)
                norm_tr(xn, xn_b, gT, dst, dst_b)

            def proj_fm(wblk, wb, col0, M, dst_fn, scale, k=8, groups=range(4)):
                for g in groups:
                    bi = bank()
                    for c in range(k):
                        T.op("pe", lambda c=c: nc.tensor.matmul(ps[bi][0:M, :], lhsT=wblk[:, c, col0:col0 + M],
                                                                rhs=hT[:, c, g * 512:(g + 1) * 512],
                                                                start=(c == 0), stop=(c == k - 1)),
                             reads=[wb] + hT_b[4 * g:4 * g + 4], writes=[ps_b[bi]], inc=(c == k - 1))
                    out, ob = dst_fn(g)
                    evac_copy("act" if g % 2 == 0 else "dve", out, ps[bi][0:M, :], [ps_b[bi]], [ob], scale)

            def proj_tm(wblk, wb, dst, dst_b, aug=False, tiles=range(16)):
                for i in tiles:
                    bi = bank()
                    for c in range(8):
                        T.op("pe", lambda c=c: nc.tensor.matmul(ps[bi][:, :], lhsT=hT[:, c, i * 128:(i + 1) * 128],
                                                                rhs=wblk[:, c, 0:512],
                                                                start=(c == 0), stop=(c == 7)),
                             reads=[wb, hT_b[i]], writes=[ps_b[bi]], inc=(c == 7))
                    if aug:
                        psv = ps[bi][:, :].rearrange("p (j e d) -> p j e d", j=4, e=2)
                        evac_copy("act", dst[:, i, :, 0, 0:64], psv[:, :, 0, :], [ps_b[bi]], [dst_b[i]])
                        evac_copy("dve", dst[:, i, :, 1, 64:128], psv[:, :, 1, :], [ps_b[bi]], [dst_b[i]])
                    else:
                        evac_copy("act" if i % 2 == 0 else "dve", dst[:, i, :], ps[bi][:, :], [ps_b[bi]], [dst_b[i]])

            with ExitStack() as esC:
                sa = lambda name, shape, dt: esC.enter_context(nc.sbuf_tensor("%s_%d" % (name, sq), list(shape), dt))
                qT = sa("qTz", [128, 8, S], BF16)
                kT = sa("kT", [128, 4, S], BF16)
                vS = sa("vS", [128, 16, 512], BF16)
                qT_b = [T.bufs_n("qT%d_" % j, 4) for j in range(4)]
                kT_b = [T.bufs_n("kT%d_" % j, 4) for j in range(4)]
                vS_b = T.bufs_n("vS", 16)

                lfn = sa("lfn", [128, 4, 512], BF16)
                ww = sa("ww", [128, 3, 512], BF16)
                car = sa("car", [128, 2, 512], F32)
                lfn_b = T.bufs_n("lfn", 4)
                ww_b = T.bufs_n("ww", 3)
                car_b = T.bufs_n("car", 2)

                with ExitStack() as esA:
                    sa2 = lambda name, shape, dt: esA.enter_context(
                        nc.sbuf_tensor("%s_%d" % (name, sq), list(shape), dt))
                    xs = sa2("xsA", [128, 2, D], F32)
                    xn = sa2("xnA", [128, 3, D], BF16)
                    stt = sa2("sttA", [128, 8], F32)
                    xs_b = T.bufs_n("xsA", 2)
                    xn_b = T.bufs_n("xnA", 3)
                    st_b = T.bufs_n("stA", 4)
                    T.op("dve", lambda: nc.vector.memset(qT[64:128, 0:8:2, :], 0.0), writes=sum(qT_b, []))
                    T.op("dve", lambda: nc.vector.memset(qT[0:64, 1:8:2, :], 0.0), writes=sum(qT_b, []))
                    wq, wqb = wload(w_in_d, 3072, 0, 8, 0, 512)
                    wk, wkb = wload(w_in_d, 3072, 0, 8, 512, 512)
                    wv, wvb = wload(w_in_d, 3072, 0, 8, 1024, 512)

                    def projA(g):
                        for j in range(4):
                            bi = bank()
                            for c in range(8):
                                T.op("pe", lambda c=c: nc.tensor.matmul(ps[bi][:, :], lhsT=wq[:, c, j * 128:(j + 1) * 128],
                                                                        rhs=hT[:, c, g * 512:(g + 1) * 512],
                                                                        start=(c == 0), stop=(c == 7)),
                                     reads=[wqb] + hT_b[4 * g:4 * g + 4], writes=[ps_b[bi]], inc=(c == 7))
                            evac_copy("act", qT[0:64, 2 * j, g * 512:(g + 1) * 512], ps[bi][0:64, :],
                                      [ps_b[bi]], [qT_b[j][g]], 0.125)
                            evac_copy("dve", qT[64:128, 2 * j + 1, g * 512:(g + 1) * 512], ps[bi][64:128, :],
                                      [ps_b[bi]], [qT_b[j][g]], 0.125)
                        for j in range(4):
                            proj_fm(wk, wkb, j * 128, 128,
                                    lambda g_, j=j: (kT[:, j, g_ * 512:(g_ + 1) * 512], kT_b[j][g_]), None,
                                    groups=[g])
                        proj_tm(wv, wvb, vS, vS_b, tiles=range(4 * g, 4 * g + 4))

                    def trA(i):
                        norm_tr(xn[:, i % 3, :], xn_b[i % 3], gmixT, hT[:, :, i * 128:(i + 1) * 128], hT_b[i])
                        if i % 4 == 3:
                            projA(i // 4)

                    pend = None
                    for i in range(16):
                        sl = i % 2
                        T.dma("sp", xs[:, sl, :], x_d.ap()[sq, i * 128:(i + 1) * 128, :], xs_s[sl],
                              writes=[xs_b[sl]])
                        norm_chain(xs[:, sl, :], xs_b[sl], stt[:, i % 4:i % 4 + 1], stt[:, 4 + i % 4:5 + i % 4],
                                   st_b[i % 4], xn[:, i % 3, :], xn_b[i % 3], xn[:, i % 3, :], xn_b[i % 3])
                        if pend is not None:
                            trA(pend)
                        pend = i
                    trA(pend)
                if sq == 0:
                    dump("hT", hT[:], hT_b, [128, 8, S], BF16)
                    dump("kT", kT[:], sum(kT_b, []), [128, 4, S], BF16)
                    dump("vS", vS[:], vS_b, [128, 16, 512], BF16)

                items = []
                for h in range(NH):
                    for g in range(4):
                        kbs = list(range(4 * g + 3, -1, -1))
                        bo = rbank("sbo", [0, 1])
                        cs = rbank("sbcar", [0, 1])
                        for idx, kb in enumerate(kbs):
                            r = kb - 4 * g
                            items.append(dict(h=h, g=g, idx=idx, kb=kb, n=len(kbs), bo=bo, cs=cs, diag=(r >= 0),
                                              c0=(128 * r if r > 0 else 0), k=len(items)))

                def sbA(it):
                    h, g, kb, c0, k = it["h"], it["g"], it["kb"], it["c0"], it["k"]
                    j, po, t0 = h // 2, (h % 2) * 64, g * 512
                    bz = 2 + k % 2
                    kap = kT[:, j, kb * 128:(kb + 1) * 128]
                    qap = qT[:, h, t0 + c0:t0 + 512]
                    rd = [kT_b[j][kb // 4], qT_b[j][g]]
                    T.op("pe", lambda: nc.tensor.matmul(ps[bz][:, c0:512], lhsT=kap, rhs=qap,
                                                        start=True, stop=not it["diag"]),
                         reads=rd, writes=[ps_b[bz]], inc=not it["diag"])
                    if it["diag"]:
                        T.op("pe", lambda: nc.tensor.matmul(ps[bz][:, c0:c0 + 128], lhsT=ident[:],
                                                            rhs=mtri[:], start=False, stop=True),
                             writes=[ps_b[bz]])

                def sbB(it):
                    c0, k = it["c0"], it["k"]
                    bz = 2 + k % 2
                    T.op("act", lambda: nc.scalar.activation(out=ps[bz][:, c0:512], in_=ps[bz][:, c0:512],
                                                             func=AF.Exp),
                         reads=[ps_b[bz]], writes=[ps_b[bz]])

                def sbB2(it):
                    c0, k = it["c0"], it["k"]
                    bz, sl = 2 + k % 2, k % 4
                    T.op("act", lambda: nc.scalar.activation(out=lfn[:, sl, c0:512], in_=ps[bz][:, c0:512],
                                                             func=AF.Ln, bias=1.0, scale=1.0),
                         reads=[ps_b[bz]], writes=[lfn_b[sl]])

                def sbC(it):
                    h, g, kb, c0, k = it["h"], it["g"], it["kb"], it["c0"], it["k"]
                    j, po, t0 = h // 2, (h % 2) * 64, g * 512
                    bt, bc, sl = 4 + k % 2, 6, k % 4
                    kap = kT[:, j, kb * 128:(kb + 1) * 128]
                    qap = qT[:, h, t0 + c0:t0 + 512]
                    rd = [kT_b[j][kb // 4], qT_b[j][g]]
                    T.op("pe", lambda: nc.tensor.matmul(ps[bt][:, c0:512], lhsT=kap, rhs=qap,
                                                        start=True, stop=False),
                         reads=rd, writes=[ps_b[bt]], inc=False)
                    if it["diag"]:
                        T.op("pe", lambda: nc.tensor.matmul(ps[bt][:, c0:c0 + 128], lhsT=ident[:],
                                                            rhs=mtri[:], start=False, stop=False),
                             writes=[ps_b[bt]], inc=False)
                    T.op("pe", lambda: nc.tensor.matmul(ps[bt][:, c0:512], lhsT=negui[:],
                                                        rhs=lfn[:, sl, c0:512], start=False, stop=True),
                         reads=[lfn_b[sl]], writes=[ps_b[bt]])
                    if it["idx"] < it["n"] - 1:
                        T.op("pe", lambda: nc.tensor.matmul(ps[bc][:, c0:512], lhsT=negones[:],
                                                            rhs=lfn[:, sl, c0:512], start=True, stop=True),
                             reads=[lfn_b[sl]], writes=[ps_b[bc]])
                    for _ in range(NDUM_SB):
                        T.op("pe", lambda: nc.tensor.matmul(ps[7][:, :], lhsT=ident[:], rhs=hT[:, 0, 0:512],
                                                            start=True, stop=True),
                             writes=[ps_b[7]], inc=False)

                def sbD(it):
                    c0, k, cs, idx = it["c0"], it["k"], it["cs"], it["idx"]
                    bt, bc, st = 4 + k % 2, 6, k % 3
                    if idx > 0:
                        T.op("dve", lambda: nc.vector.tensor_tensor(out=ps[bt][:, c0:512], in0=ps[bt][:, c0:512],
                                                                    in1=car[:, cs, c0:512], op=ALU.add),
                             reads=[ps_b[bt], car_b[cs]], writes=[ps_b[bt]])
                    if idx < it["n"] - 1:
                        if idx == 0:
                            if c0 > 0:
                                T.op("dve", lambda: nc.vector.memset(car[:, cs, 0:c0], 0.0), writes=[car_b[cs]])
                            T.op("dve", lambda: nc.vector.tensor_copy(out=car[:, cs, c0:512], in_=ps[bc][:, c0:512]),
                                 reads=[ps_b[bc]], writes=[car_b[cs]])
                        else:
                            T.op("dve", lambda: nc.vector.tensor_tensor(out=car[:, cs, c0:512],
                                                                        in0=ps[bc][:, c0:512],
                                                                        in1=car[:, cs, c0:512], op=ALU.add),
                                 reads=[ps_b[bc], car_b[cs]], writes=[car_b[cs]])

                def sbE(it):
                    c0, k, idx = it["c0"], it["k"], it["idx"]
                    bt, st, sw = 4 + k % 2, k % 3, k % 3
                    T.op("act", lambda: nc.scalar.activation(out=ww[:, sw, c0:512], in_=ps[bt][:, c0:512],
                                                             func=AF.Exp),
                         reads=[ps_b[bt]], writes=[ww_b[sw]])

                def sbF(it):
                    h, g, kb, c0, k, idx, bo = it["h"], it["g"], it["kb"], it["c0"], it["k"], it["idx"], it["bo"]
                    j, po, t0, sw = h // 2, (h % 2) * 64, g * 512, k % 3
                    last = idx == it["n"] - 1
                    T.op("pe", lambda: nc.tensor.matmul(ps[bo][:, c0:512],
                                                        lhsT=vS[:, kb, j * 128:(j + 1) * 128],
                                                        rhs=ww[:, sw, c0:512], start=(idx == 0), stop=last,
                                                        skip_group_check=True),
                         reads=[vS_b[kb], ww_b[sw]], writes=[ps_b[bo]])
                    if last:
                        T.op("dve", lambda: nc.vector.tensor_copy(out=oT_sb[po:po + 64, j, t0:t0 + 512],
                                                                  in_=ps[bo][po:po + 64, :]),
                             reads=[ps_b[bo]], writes=[oTsb_b[j][g]])

                NI = len(items)
                for k in range(NI + 3):
                    if k < NI:
                        sbA(items[k])
                        sbB(items[k])
                    if 0 <= k - 1 < NI:
                        sbC(items[k - 1])
                        sbD(items[k - 1])
                    if 0 <= k - 2 < NI:
                        sbE(items[k - 2])
                    if k < NI:
                        sbB2(items[k])
                    if 0 <= k - 3 < NI:
                        sbF(items[k - 3])
                T.barrier(skip=("pool",), keep=wkeep)
            if sq == 0:
                dump("oT_sb", oT_sb[:], sum(oTsb_b, []), [128, 4, S], BF16)

            with ExitStack() as esD:
                sa = lambda name, shape, dt: esD.enter_context(nc.sbuf_tensor("%s_%d" % (name, sq), list(shape), dt))
                qa = sa("qa", [128, 4, S], BF16)
                ka = sa("ka", [128, 4, S], BF16)
                vM = sa("vM", [128, 16, 4, 2, 128], BF16)
                TB = sa("TB", [128, 8, TBW], BF16)
                tb_b = T.buf("TB")
                T.dma("sp", TB[:].rearrange("p h j -> p (h j)"), tbd_d.ap(), tb_s, writes=[tb_b])
                pp = sa("pp", [128, 4, 512], BF16)
                rden = sa("rden", [128, 2, 512], F32)
                kbf = sa("kbf", [128, 4, 8], F32)
                kbT = sa("kbT", [128, 4, 8], BF16)
                gsb = sa("gsb", [128, 2, 4, 8], F32)
                m8 = sa("m8", [128, 2, 4, 8], F32)
                sel = sa("sel", [128, 2, 4, 8], F32)
                vM_b = T.bufs_n("vM", 16)
                pp_b = T.bufs_n("pp", 4)
                rden_b = T.bufs_n("rden", 2)
                kb_b = T.buf("kbar")
                gs_b = T.bufs_n("gs", 2)
                sel_b = T.bufs_n("selb", 2)
                mp_b = T.bufs_n("maskpad", 3)

                qa_b = [T.bufs_n("qa%d_" % hl, 4) for hl in range(4)]
                ka_b = [T.bufs_n("ka%d_" % hl, 4) for hl in range(4)]
                qm_b = T.bufs_n("qm", 16)
                kind_b = T.buf("kind")
                for hl in range(4):
                    T.dma("sp", ka[64:72, hl, :], kindd_d.ap(), kind_s, writes=[kind_b])
                T.op("dve", lambda: nc.vector.memset(qa[64:72, :, 0:1024], 0.0), writes=qm_b[0:8])
                for hh in range(2):
                    wq, wqb = wload(w_in_d, 3072, 0, 8, 1536, 512)
                    wk, wkb = wload(w_in_d, 3072, 0, 8, 2048, 512)
                    for (wblk, wb, dstt, dst_b2, scl) in ((wq, wqb, qa, qa_b, 0.125), (wk, wkb, ka, ka_b, None)):
                        for pr in range(2):
                            col0 = (hh * 4 + 2 * pr) * 64
                            for g in range(4):
                                bi = bank()
                                for c in range(8):
                                    T.op("pe", lambda c=c: nc.tensor.matmul(
                                        ps[bi][:, :], lhsT=wblk[:, c, col0:col0 + 128],
                                        rhs=hT[:, c, g * 512:(g + 1) * 512], start=(c == 0), stop=(c == 7)),
                                         reads=[wb] + hT_b[4 * g:4 * g + 4], writes=[ps_b[bi]], inc=(c == 7))
                                evac_copy("dve", dstt[0:64, 2 * pr, g * 512:(g + 1) * 512], ps[bi][0:64, :],
                                          [ps_b[bi]], [dst_b2[2 * pr][g]], scl)
                                evac_copy("act", dstt[0:64, 2 * pr + 1, g * 512:(g + 1) * 512], ps[bi][64:128, :],
                                          [ps_b[bi]], [dst_b2[2 * pr + 1][g]], scl)
                    if hh == 0:
                        wv, wvb = wload(w_in_d, 3072, 0, 8, 2560, 512)
                        T.op("dve", lambda: nc.vector.memset(vM[:, :, :, 0, 64:128], 1.0), writes=vM_b)
                        T.op("dve", lambda: nc.vector.memset(vM[:, :, :, 1, 0:64], 1.0), writes=vM_b)
                        proj_tm(wv, wvb, vM, vM_b, aug=True)
                    for hl in range(4):
                        T.op("dve", lambda hl=hl: nc.vector.tensor_reduce(
                            out=kbf[0:64, hl, :], in_=ka[0:64, hl, :].rearrange("p (n s) -> p n s", s=256),
                            axis=AX.X, op=ALU.add), reads=ka_b[hl], writes=[kb_b])
                    T.op("dve", lambda: nc.vector.tensor_scalar(out=kbT[0:64, :, :], in0=kbf[0:64, :, :],
                                                                scalar1=1.0 / 256, scalar2=None, op0=ALU.mult),
                         reads=[kb_b], writes=[kb_b])
                    def gate1(i):
                        c = i // 2
                        gsl = i % 2
                        bg = 7
                        for hl in range(4):
                            T.op("pe", lambda hl=hl: nc.tensor.matmul(ps[bg][:, hl * 8:(hl + 1) * 8],
                                                                      lhsT=qa[0:64, hl, i * 128:(i + 1) * 128],
                                                                      rhs=kbT[0:64, hl, :], start=True, stop=True),
                                 reads=[qa_b[hl][i // 4], kb_b], writes=[ps_b[bg]], inc=(hl == 3))
                        T.op("dve", lambda: nc.vector.tensor_copy(
                            out=gsb[:, gsl, :, :], in_=ps[bg][:, 0:32].rearrange("p (h n) -> p h n", h=4)),
                             reads=[ps_b[bg]], writes=[gs_b[gsl]])
                        if c < 8:
                            T.op("dve", lambda: nc.vector.memset(gsb[:, gsl, :, c:8], -1e30), writes=[gs_b[gsl]])
                        for hl in range(4):
                            T.op("dve", lambda hl=hl: nc.vector.max(out=m8[:, gsl, hl, :], in_=gsb[:, gsl, hl, :]),
                                 reads=[gs_b[gsl]], writes=[gs_b[gsl]])
                        for hl in range(4):
                            T.op("pool", lambda hl=hl: nc.gpsimd.tensor_scalar(
                                out=sel[:, gsl, hl, :], in0=gsb[:, gsl, hl, :], scalar1=m8[:, gsl, hl, 2:3],
                                scalar2=None, op0=ALU.is_ge), reads=[gs_b[gsl]], writes=[sel_b[gsl]])
                        ms = i % 3
                        T.op("pool", lambda: nc.gpsimd.tensor_scalar(
                            out=maskpad[:, ms, :, 64:72], in0=sel[:, gsl, :, :], scalar1=-1.0, scalar2=-NEG,
                            op0=ALU.add, op1=ALU.mult), reads=[sel_b[gsl]], writes=[mp_b[ms]])
                        T.op("pool", lambda: nc.gpsimd.memset(maskpad[:, ms, :, 64 + c:72], 0.0), writes=[mp_b[ms]])

                    def gate2(i):
                        bm = 7
                        ms = i % 3
                        for hl in range(4):
                            T.op("pe", lambda hl=hl: nc.tensor.matmul(ps[bm][0:72, hl * 128:(hl + 1) * 128],
                                                                      lhsT=maskpad[:, ms, hl, :], rhs=ident[:],
                                                                      start=True, stop=True),
                                 reads=[mp_b[ms]], writes=[ps_b[bm]], inc=(hl == 3))
                        T.op("act", lambda: nc.scalar.copy(
                            out=qa[64:72, :, i * 128:(i + 1) * 128],
                            in_=ps[bm][64:72, :].rearrange("p (h t) -> p h t", h=4)),
                             reads=[ps_b[bm]], writes=[qm_b[i]])

                    gsched = {}
                    g1s = {8: 12, 9: 20, 10: 28, 11: 36, 12: 48, 13: 58, 14: 68, 15: 78}
                    g2s = {8: 24, 9: 32, 10: 40, 11: 47, 12: 60, 13: 70, 14: 80, 15: 90}
                    for i in range(8, 16):
                        gsched.setdefault(g1s[i], []).append(lambda i=i: gate1(i))
                        gsched.setdefault(g2s[i], []).append(lambda i=i: gate2(i))
                    mits = []
                    g2 = 0
                    for m in range(4):
                        for hl in range(4):
                            bn = rbank("mbn", [0, 1])
                            nt = 4 * m + 4
                            for kb in range(nt):
                                mits.append(dict(hl=hl, m=m, kb=kb, ti=kb, nt=nt, bn=bn, k=len(mits), g2=g2))
                            g2 += 1

                    def mgeom(it):
                        m, kb = it["m"], it["kb"]
                        t0 = 512 * m
                        delta = t0 - 128 * kb
                        c0 = max(0, -delta)
                        return t0, delta, c0, delta + c0, 512 - c0, delta <= 128

                    def mbA(it):
                        hl, m, kb, k = it["hl"], it["m"], it["kb"], it["k"]
                        h = hh * 4 + hl
                        t0, delta, c0, j0, ncol, near = mgeom(it)
                        bl = 2 + k % 5
                        sl = k % 4
                        T.op("pe", lambda: nc.tensor.matmul(
                            ps[bl][:, c0:512], lhsT=ka[0:72, hl, kb * 128:(kb + 1) * 128],
                            rhs=qa[0:72, hl, t0 + c0:t0 + 512], start=True, stop=not near),
                             reads=[ka_b[hl][kb // 4], kind_b, qa_b[hl][m]] + qm_b[4 * m:4 * m + 4],
                             writes=[ps_b[bl]], inc=not near)
                        if near:
                            T.op("pe", lambda: nc.tensor.matmul(
                                ps[bl][:, c0:512], lhsT=ident[:], rhs=TB[:, h, j0:j0 + ncol],
                                start=False, stop=True), reads=[tb_b], writes=[ps_b[bl]])
                            T.op("act", lambda: nc.scalar.activation(
                                out=pp[:, sl, c0:512], in_=ps[bl][:, c0:512], func=AF.Exp),
                                 reads=[ps_b[bl]], writes=[pp_b[sl]])
                        else:
                            T.op("act", lambda: nc.scalar.activation(
                                out=pp[:, sl, c0:512], in_=ps[bl][:, c0:512], func=AF.Exp,
                                bias=b31[:, h:h + 1], scale=1.0),
                                 reads=[ps_b[bl]], writes=[pp_b[sl]])

                    def mbD(it):
                        for _ in range(NDUM_MB):
                            T.op("pe", lambda: nc.tensor.matmul(ps[7][:, 0:256], lhsT=ident[:], rhs=hT[:, 0, 0:256],
                                                                start=True, stop=True),
                                 writes=[ps_b[7]], inc=False)

                    def mbC(it):
                        hl, m, kb, k, bn = it["hl"], it["m"], it["kb"], it["k"], it["bn"]
                        h = hh * 4 + hl
                        j = h // 2
                        po = (h % 2) * 64
                        t0, delta, c0, j0, ncol, near = mgeom(it)
                        sl = k % 4
                        first = it["ti"] == 0
                        lastt = it["ti"] == it["nt"] - 1
                        T.op("pe", lambda: nc.tensor.matmul(
                            ps[bn][:, c0:512], lhsT=vM[:, kb, j, h % 2, :],
                            rhs=pp[:, sl, c0:512], start=first, stop=lastt, skip_group_check=True),
                             reads=[vM_b[kb], pp_b[sl]], writes=[ps_b[bn]], inc=True)
                        if lastt:
                            rs = it["g2"] % 2
                            T.op("act", lambda: nc.scalar.copy(out=rden[po:po + 64, rs, :],
                                                               in_=ps[bn][64 - po:128 - po, :]),
                                 reads=[ps_b[bn]], writes=[rden_b[rs]])
                            T.op("dve", lambda: nc.vector.reciprocal(out=rden[po:po + 64, rs, :],
                                                                     in_=rden[po:po + 64, rs, :]),
                                 reads=[rden_b[rs]], writes=[rden_b[rs]])
                            T.op("dve", lambda: nc.vector.tensor_tensor(
                                out=oT_mb[po:po + 64, j, t0:t0 + 512], in0=ps[bn][po:po + 64, :],
                                in1=rden[po:po + 64, rs, :], op=ALU.mult),
                                 reads=[ps_b[bn], rden_b[rs]], writes=[oTmb_b[j][m]])

                    NM = len(mits)
                    for k in range(NM + 2):
                        for fn_ in gsched.get(k, []):
                            fn_()
                        if k < NM:
                            mbA(mits[k])
                            mbD(mits[k])
                        if 0 <= k - 2 < NM:
                            mbC(mits[k - 2])
                T.barrier(skip=("pool",), keep=wkeep)
            if sq == 0:
                dump("oT_mb", oT_mb[:], sum(oTmb_b, []), [128, 4, S], BF16)

            with ExitStack() as esE:
                sa = lambda name, shape, dt: esE.enter_context(nc.sbuf_tensor("%s_%d" % (name, sq), list(shape), dt))
                xs = sa("xsE", [128, 4, D], F32)
                gfin = sa("gfin", [128, D], F32)
                gfin_b = T.buf("gfin")
                T.dma("sp", gfin[:], bass.AP(fin_d, 0, [[0, 128], [1, D]]), gfin_s, writes=[gfin_b])
                xn = sa("xnE", [128, 3, D], BF16)
                junk = sa("junkE", [128, D], F32)
                stt = sa("sttE", [128, 8], F32)
                actT = sa("actT", [128, 8, 512], BF16)
                aT = sa("aT", [128, NFC, 512], BF16)
                fscr = sa("fscr", [128, 4, 512], F32)
                pT = sa("pT", [128, 2, 512], BF16)
                pt = sa("pt", [128, 4, PLE], F32)
                pb = sa("pb", [128, 4, PLE], BF16)
                xs_b = T.bufs_n("xsE", 4)
                xn_b = T.bufs_n("xnE", 3)
                junk_b = T.buf("junkE")
                st_b = T.bufs_n("stE", 4)
                actT_b = T.bufs_n("actT", 4)
                aT_b = T.bufs_n("aT", NFC)
                fs_b = T.bufs_n("fscr", 4)
                pT_b = T.bufs_n("pT", 4)
                pt_b = T.bufs_n("pt", 4)
                pb_b = T.bufs_n("pb", 4)
                fsi = [0]

                def fs():
                    i = fsi[0]
                    fsi[0] = (i + 1) % 4
                    return i

                for g in range(4):
                    tok = slice(g * 512, (g + 1) * 512)
                    for i in range(4):
                        r0 = g * 512 + i * 128
                        T.dma("sp", xs[:, i, :], x_d.ap()[sq, r0:r0 + 128, :], xs_s[i], writes=[xs_b[i]])
                    for i in range(4):
                        r0 = g * 512 + i * 128
                        T.dma("sp", pt[:, i, :], p_d.ap()[sq, r0:r0 + 128, :], pt_s[i], writes=[pt_b[i]])
                        T.op("dve", lambda: nc.vector.tensor_copy(out=pb[:, i, :], in_=pt[:, i, :]),
                             reads=[pt_b[i]], writes=[pb_b[i]])
                    for hf in range(2):
                        wgs, wgsb = wload(w_gate_d, 2048, 0, 8, hf * 512, 512)
                        wgm, wgmb = wload(w_gate_d, 2048, 0, 8, 1024 + hf * 512, 512)
                        if hf == 0:
                            wbs, wbsb = wload(w_bsb_d, D, 0, 4, 0, 1024)
                            wbm, wbmb = wload(w_bmb_d, D, 0, 4, 0, 1024)
                        for dq in range(4):
                            dc = hf * 4 + dq
                            bys, bym, bgs, bgm = bank(), bank(), bank(), bank()
                            for (bb, wblk, wb) in ((bgs, wgs, wgsb), (bgm, wgm, wgmb)):
                                for cc in range(8):
                                    T.op("pe", lambda cc=cc, bb=bb, wblk=wblk: nc.tensor.matmul(
                                        ps[bb][:, :], lhsT=wblk[:, cc, dq * 128:(dq + 1) * 128], rhs=hT[:, cc, tok],
                                        start=(cc == 0), stop=(cc == 7)),
                                         reads=[wb] + hT_b[4 * g:4 * g + 4], writes=[ps_b[bb]], inc=(cc == 7))
                            for (bb, wblk, wb, oT, oTb) in ((bys, wbs, wbsb, oT_sb, oTsb_b),
                                                            (bym, wbm, wbmb, oT_mb, oTmb_b)):
                                for jj in range(4):
                                    T.op("pe", lambda jj=jj, bb=bb, wblk=wblk, oT=oT: nc.tensor.matmul(
                                        ps[bb][:, :], lhsT=wblk[:, jj, dc * 128:(dc + 1) * 128], rhs=oT[:, jj, tok],
                                        start=(jj == 0), stop=(jj == 3)),
                                         reads=[wb, oTb[jj][g]], writes=[ps_b[bb]], inc=(jj == 3))
                            f1, f2 = fs(), fs()
                            T.op("act", lambda: nc.scalar.activation(out=fscr[:, f1, :], in_=ps[bgs][:, :],
                                                                     func=AF.Sigmoid, bias=bgT[:, dc:dc + 1],
                                                                     scale=1.0),
                                 reads=[ps_b[bgs]], writes=[fs_b[f1]])
                            T.op("act", lambda: nc.scalar.activation(out=fscr[:, f2, :], in_=ps[bgm][:, :],
                                                                     func=AF.Sigmoid, bias=bgT[:, 8 + dc:9 + dc],
                                                                     scale=1.0),
                                 reads=[ps_b[bgm]], writes=[fs_b[f2]])
                            T.op("dve", lambda: nc.vector.tensor_tensor(out=fscr[:, f1, :], in0=fscr[:, f1, :],
                                                                        in1=ps[bys][:, :], op=ALU.mult),
                                 reads=[fs_b[f1], ps_b[bys]], writes=[fs_b[f1]])
                            T.op("dve", lambda: nc.vector.tensor_tensor(out=fscr[:, f2, :], in0=fscr[:, f2, :],
                                                                        in1=ps[bym][:, :], op=ALU.mult),
                                 reads=[fs_b[f2], ps_b[bym]], writes=[fs_b[f2]])
                            T.op("dve", lambda: nc.vector.tensor_tensor(out=actT[:, dc, :], in0=fscr[:, f1, :],
                                                                        in1=fscr[:, f2, :], op=ALU.add),
                                 reads=[fs_b[f1], fs_b[f2]], writes=actT_b)
                    wo2 = [wload(w_out_d, D, 0, 8, cb * 512, 512) for cb in range(2)]
                    pend = None
                    for i in range(4):
                        for cb in range(2):
                            wo, wob = wo2[cb]
                            bi = bank()
                            for dc in range(8):
                                T.op("pe", lambda dc=dc: nc.tensor.matmul(
                                    ps[bi][:, :], lhsT=actT[:, dc, i * 128:(i + 1) * 128], rhs=wo[:, dc, :],
                                    start=(dc == 0), stop=(dc == 7)),
                                     reads=[wob, actT_b[i]], writes=[ps_b[bi]], inc=(dc == 7))
                            T.op("dve", lambda: nc.vector.tensor_tensor(
                                out=xs[:, i, cb * 512:(cb + 1) * 512], in0=xs[:, i, cb * 512:(cb + 1) * 512],
                                in1=ps[bi][:, :], op=ALU.add),
                                 reads=[xs_b[i], ps_b[bi]], writes=[xs_b[i]])
                        norm_chain(xs[:, i, :], xs_b[i], stt[:, i:i + 1], stt[:, 4 + i:5 + i], st_b[i],
                                   junk[:], junk_b, xn[:, i % 3, :], xn_b[i % 3])
                        if i >= 2:
                            norm_tr(xn[:, (i - 2) % 3, :], xn_b[(i - 2) % 3], gffnT,
                                    actT[:, :, (i - 2) * 128:(i - 1) * 128], actT_b[i - 2])
                    for pend in (2, 3):
                        norm_tr(xn[:, pend % 3, :], xn_b[pend % 3], gffnT,
                                actT[:, :, pend * 128:(pend + 1) * 128], actT_b[pend])
                    if sq == 0 and g == 0:
                        dump("x1", xs[:], xs_b, [128, 4, D], F32)
                    for fb in range(6):
                        ncol = 512 if fb < 5 else 256
                        wg, wgb = wload(w_fg_d, FF, 0, 8, fb * 512, ncol)
                        wu, wub = wload(w_fu_d, FF, 0, 8, fb * 512, ncol)
                        for fc in range(ncol // 128):
                            f = fb * 4 + fc
                            bgt, but = bank(), bank()
                            for (bb, wblk, wb) in ((bgt, wg, wgb), (but, wu, wub)):
                                for cc in range(8):
                                    T.op("pe", lambda cc=cc, bb=bb, wblk=wblk: nc.tensor.matmul(
                                        ps[bb][:, :], lhsT=wblk[:, cc, fc * 128:(fc + 1) * 128], rhs=actT[:, cc, :],
                                        start=(cc == 0), stop=(cc == 7)),
                                         reads=[wb] + actT_b, writes=[ps_b[bb]], inc=(cc == 7))
                            f1 = fs()
                            T.op("act", lambda: nc.scalar.activation(out=fscr[:, f1, :], in_=ps[bgt][:, :],
                                                                     func=AF.Silu),
                                 reads=[ps_b[bgt]], writes=[fs_b[f1]])
                            T.op("dve", lambda: nc.vector.tensor_tensor(out=aT[:, f, :], in0=fscr[:, f1, :],
                                                                        in1=ps[but][:, :], op=ALU.mult),
                                 reads=[fs_b[f1], ps_b[but]], writes=[aT_b[f]])
                    for i in range(4):
                        s2 = i
                        bi = bank()
                        pst = ps[bi][:].bitcast(BF16)
                        for cc in range(2):
                            T.op("pe", lambda cc=cc: nc.tensor.transpose(out=pst[:, cc * 128:(cc + 1) * 128],
                                                                         in_=pb[:, s2, cc * 128:(cc + 1) * 128],
                                                                         identity=ident[:]),
                                 reads=[pb_b[s2]], writes=[ps_b[bi]], inc=(cc == 1))
                        T.op("act", lambda: nc.scalar.copy(
                            out=pT[:, :, i * 128:(i + 1) * 128],
                            in_=pst[:, 0:256].rearrange("p (c t) -> p c t", c=2)),
                             reads=[ps_b[bi]], writes=[pT_b[i]])
                    pend = None
                    for cb in range(2):
                        blks = []
                        for ks in range(3):
                            nk = 8 if ks < 2 else 6
                            blks.append(wload(w_fd_d, D, ks * 8, nk, cb * 512, 512))
                        for i in range(4):
                            bi = bank()
                            for f in range(NFC):
                                wblk, wb = blks[f // 8]
                                T.op("pe", lambda f=f, wblk=wblk: nc.tensor.matmul(
                                    ps[bi][:, :], lhsT=aT[:, f, i * 128:(i + 1) * 128], rhs=wblk[:, f % 8, :],
                                    start=(f == 0), stop=(f == NFC - 1)),
                                     reads=[wb, aT_b[f]], writes=[ps_b[bi]], inc=(f == NFC - 1))
                            T.op("dve", lambda: nc.vector.tensor_tensor(
                                out=xs[:, i, cb * 512:(cb + 1) * 512], in0=xs[:, i, cb * 512:(cb + 1) * 512],
                                in1=ps[bi][:, :], op=ALU.add),
                                 reads=[xs_b[i], ps_b[bi]], writes=[xs_b[i]])
                            if cb == 1:
                                norm_chain(xs[:, i, :], xs_b[i], stt[:, i:i + 1], stt[:, 4 + i:5 + i], st_b[i],
                                           junk[:], junk_b, xn[:, i % 3, :], xn_b[i % 3])
                                if pend is not None:
                                    norm_tr(xn[:, pend % 3, :], xn_b[pend % 3], gpleT,
                                            actT[:, :, pend * 128:(pend + 1) * 128], actT_b[pend])
                                pend = i
                    if sq == 0 and g == 0:
                        dump("x2", xs[:], xs_b, [128, 4, D], F32)
                    wpp, wppb = wload(w_pp_d, D, 0, 2, 0, 1024)
                    wpg2 = [wload(w_pg_d, D, 0, 8, cb * 512, 512) for cb in range(2)]
                    for i in range(4):
                        if i == pend:
                            norm_tr(xn[:, pend % 3, :], xn_b[pend % 3], gpleT,
                                    actT[:, :, pend * 128:(pend + 1) * 128], actT_b[pend])
                        for cb in range(2):
                            wpg, wpgb = wpg2[cb]
                            bgt, bpp = bank(), bank()
                            for cc in range(8):
                                T.op("pe", lambda cc=cc: nc.tensor.matmul(
                                    ps[bgt][:, :], lhsT=actT[:, cc, i * 128:(i + 1) * 128], rhs=wpg[:, cc, :],
                                    start=(cc == 0), stop=(cc == 7)),
                                     reads=[wpgb, actT_b[i]], writes=[ps_b[bgt]], inc=(cc == 7))
                            for cc in range(2):
                                T.op("pe", lambda cc=cc: nc.tensor.matmul(
                                    ps[bpp][:, :], lhsT=pT[:, cc, i * 128:(i + 1) * 128],
                                    rhs=wpp[:, cc, cb * 512:(cb + 1) * 512], start=(cc == 0), stop=(cc == 1)),
                                     reads=[wppb, pT_b[i]], writes=[ps_b[bpp]], inc=(cc == 1))
                            f1 = fs()
                            T.op("act", lambda: nc.scalar.activation(out=fscr[:, f1, :], in_=ps[bgt][:, :],
                                                                     func=AF.Sigmoid),
                                 reads=[ps_b[bgt]], writes=[fs_b[f1]])
                            T.op("dve", lambda: nc.vector.tensor_tensor(out=fscr[:, f1, :], in0=fscr[:, f1, :],
                                                                        in1=ps[bpp][:, :], op=ALU.mult),
                                 reads=[fs_b[f1], ps_b[bpp]], writes=[fs_b[f1]])
                            T.op("dve", lambda: nc.vector.tensor_tensor(
                                out=xs[:, i, cb * 512:(cb + 1) * 512], in0=xs[:, i, cb * 512:(cb + 1) * 512],
                                in1=fscr[:, f1, :], op=ALU.add),
                                 reads=[xs_b[i], fs_b[f1]], writes=[xs_b[i]])
                    for i in range(4):
                        norm_rstd(xs[:, i, :], xs_b[i], stt[:, i:i + 1], stt[:, 4 + i:5 + i], st_b[i],
                                  junk[:], junk_b)
                        T.op("dve", lambda: nc.vector.scalar_tensor_tensor(
                            out=xs[:, i, :], in0=xs[:, i, :], scalar=stt[:, 4 + i:5 + i], in1=gfin[:],
                            op0=ALU.mult, op1=ALU.mult),
                             reads=[xs_b[i], st_b[i], gfin_b], writes=[xs_b[i]])
                        r0 = g * 512 + i * 128
                        T.dma("sp", y_d.ap()[sq, r0:r0 + 128, :], xs[:, i, :], out_s[i], reads=[xs_b[i]])
                T.barrier(skip=("pool",), keep=wkeep)
        T.barrier()
    return nc, dbg_t


_W_NAMES = ["ln_mix_g", "w_in", "w_gate", "b_gate", "w_branch_sb", "w_branch_moba", "w_out",
            "ln_ffn_g", "w_ffn_gate", "w_ffn_up", "w_ffn_down", "ln_ple_g", "w_ple_gate", "w_ple_proj"]


def make_in_maps(inputs, ncores, nseq):
    shared = {k: np.ascontiguousarray(np.asarray(inputs[k], np.float32)[0]) for k in _W_NAMES}
    shared["rel_bias"] = np.ascontiguousarray(np.asarray(inputs["rel_bias"], np.float32))
    shared["final_g"] = np.ascontiguousarray(np.asarray(inputs["final_g"], np.float32))
    shared.update(_consts())
    x = np.asarray(inputs["x"], np.float32)
    p = np.asarray(inputs["p"], np.float32)[0]
    maps = []
    for c in range(ncores):
        m = dict(shared)
        m["x"] = np.ascontiguousarray(x[c * nseq:(c + 1) * nseq])
        m["p"] = np.ascontiguousarray(p[c * nseq:(c + 1) * nseq])
        maps.append(m)
    return maps


def kernel(**inputs):
    nseq = BATCH // NCORES
    nc, _ = build(nseq)
    maps = make_in_maps(inputs, NCORES, nseq)
    res = run_bass_kernel_spmd(nc, maps, core_ids=list(range(NCORES)))
    out = np.concatenate([np.asarray(r["y"]) for r in res.results], axis=0)
    return out.astype(np.float32)
```
